# Optimizing a Trainium2 kernel written in Bass

```python
import functools
import jax, jax.numpy as jnp
from jax import lax
import numpy as np

D_MODEL = 2048
BATCH = 8
SEQ = 4096
DEPTH = 1
DEC_BATCH = 16
DEC_SEQ = 16
PAST_LEN = 4096

CHUNK = 64
N_RET_HEADS = 8
RET_DK = 128
RET_DV = 256
RET_QK = N_RET_HEADS * RET_DK
RET_V = N_RET_HEADS * RET_DV
N_ATT_HEADS = 8
ATT_HEAD_DIM = 128
ATT_W = N_ATT_HEADS * ATT_HEAD_DIM
ATT_LEFT_CHUNKS = 8
ATT_REACH = ATT_LEFT_CHUNKS * CHUNK
BAND = ATT_REACH + CHUNK
REL_CLIP = 128
D_FF = 5632
CONV_W = 3
ROPE_BASE = 10000.0
EPS = 1e-6
IN_SIZES = (RET_QK, RET_QK, RET_V, RET_V, ATT_W, ATT_W, ATT_W, D_MODEL, D_MODEL)
IN_WIDTH = 2 * RET_QK + 2 * RET_V + 3 * ATT_W + 2 * D_MODEL

kernel_name = 'hybrid_retention_chunkband_convffn_step'


def rms_norm(x, g=None):
    xf = x.astype(jnp.float32)
    y = (xf * lax.rsqrt(jnp.mean(xf * xf, axis=-1, keepdims=True) + EPS)).astype(x.dtype)
    return y if g is None else y * g


def rotary(x, pos):
    half = x.shape[-1] // 2
    inv = ROPE_BASE ** (-jnp.arange(half, dtype=jnp.float32) / half)
    ang = pos.astype(jnp.float32)[:, None] * inv[None, :]
    cos = jnp.cos(ang)[:, None, :].astype(x.dtype)
    sin = jnp.sin(ang)[:, None, :].astype(x.dtype)
    x1, x2 = x[..., :half], x[..., half:]
    return jnp.concatenate([x1 * cos - x2 * sin, x1 * sin + x2 * cos], axis=-1)


def retention_block(state, q, k, v, log_gamma):
    L = q.shape[1]
    dt = q.dtype
    t = jnp.arange(L, dtype=jnp.float32)
    decay_in = jnp.exp(log_gamma[:, None, None] * jnp.abs(t[:, None] - t[None, :])).astype(dt)
    decay_q = jnp.exp((t[:, None] + 1.0) * log_gamma[None, :]).astype(dt)
    decay_k = jnp.exp((L - 1.0 - t)[:, None] * log_gamma[None, :]).astype(dt)
    decay_s = jnp.exp(L * log_gamma).astype(dt)
    s = jnp.einsum('bnhd,bmhd->bhnm', q, k) * decay_in[None]
    o = (jnp.einsum('bhnm,bmhe->bnhe', s, v)
         + jnp.einsum('bnhd,bhde->bnhe', q, state) * decay_q[None, :, :, None])
    new_state = (decay_s[None, :, None, None] * state
                 + jnp.einsum('bmhd,bmhe->bhde', k * decay_k[None, :, :, None], v))
    return new_state, o


def retention_prompt(q, k, v, log_gamma):
    B, S, H, dk = q.shape
    nc = S // CHUNK

    def to_chunks(t):
        return t.reshape(B, nc, CHUNK, H, t.shape[-1]).swapaxes(0, 1)

    s0 = jnp.zeros((B, H, dk, v.shape[-1]), q.dtype)
    s_fin, o = lax.scan(lambda st, xs: retention_block(st, xs[0], xs[1], xs[2], log_gamma),
                        s0, (to_chunks(q), to_chunks(k), to_chunks(v)))
    return o.swapaxes(0, 1).reshape(B, S, H, v.shape[-1]), s_fin


def retention_sample(q, k, v, state, log_gamma):
    new_state, o = retention_block(state, q, k, v, log_gamma)
    return o, new_state


def band_attention(q, k, v, q_pos, k_pos, k_valid, rel_bias):
    s = jnp.einsum('bqhd,bkhd->bhqk', q, k).astype(jnp.float32) * (q.shape[-1] ** -0.5)
    rel = jnp.clip(k_pos[None, :] - q_pos[:, None], -REL_CLIP, REL_CLIP) + REL_CLIP
    s = s + rel_bias[:, rel].astype(jnp.float32)[None]
    s = jnp.where(k_valid[None, None, None, :], s, -1e30)
    p = jax.nn.softmax(s, axis=-1).astype(v.dtype)
    return jnp.einsum('bhqk,bkhd->bqhd', p, v)


def attention_prompt(q, k, v, rel_bias):
    B, S, H, dh = q.shape
    nc = S // CHUNK
    pad = jnp.zeros((B, ATT_REACH, H, dh), k.dtype)
    kp = jnp.concatenate([pad, k], axis=1)
    vp = jnp.concatenate([pad, v], axis=1)

    def one_chunk(c):
        start = c * CHUNK
        qc = lax.dynamic_slice_in_dim(q, start, CHUNK, axis=1)
        kb = lax.dynamic_slice_in_dim(kp, start, BAND, axis=1)
        vb = lax.dynamic_slice_in_dim(vp, start, BAND, axis=1)
        q_pos = start + jnp.arange(CHUNK)
        k_pos = start - ATT_REACH + jnp.arange(BAND)
        return band_attention(qc, kb, vb, q_pos, k_pos, k_pos >= 0, rel_bias)

    o = lax.map(one_chunk, jnp.arange(nc))
    keep = min(ATT_REACH, S)
    return o.swapaxes(0, 1).reshape(B, S, H, dh), k[:, S - keep:], v[:, S - keep:]


def attention_sample(q, k, v, cache_k, cache_v, rel_bias):
    L = q.shape[1]
    P = cache_k.shape[1]
    kb = jnp.concatenate([cache_k, k], axis=1)
    vb = jnp.concatenate([cache_v, v], axis=1)
    q_pos = PAST_LEN + jnp.arange(L)
    k_pos = jnp.concatenate([PAST_LEN - P + jnp.arange(P), q_pos])
    o = band_attention(q, kb, vb, q_pos, k_pos, k_pos >= PAST_LEN - ATT_REACH, rel_bias)
    return o, k, v


def conv_ffn(u, conv_buf, w_up, conv_w, conv_b, w_down):
    L = u.shape[1]
    up = u @ w_up
    ext = jnp.concatenate([conv_buf.astype(up.dtype), up], axis=1)
    h = conv_b + conv_w[CONV_W - 1] * ext[:, CONV_W - 1:]
    for j in range(CONV_W - 1):
        h = h + conv_w[j] * ext[:, j:j + L]
    value, gate = jnp.split(h, 2, axis=-1)
    out = (jax.nn.gelu(gate, approximate=True) * value) @ w_down
    return out, ext[:, L:]


def layer_step(x, c, pos, ret_mix, att_mix, conv_buf, w_ada, b_ada, g_pre1, w_in, w_br_ret,
               w_br_att, w_out, g_post1, g_pre2, w_up, conv_w, conv_b, w_down, g_post2):
    B, L, _ = x.shape
    mod = (jax.nn.silu(c) @ w_ada + b_ada)[:, None, :]
    sh1, sc1, gt1, sh2, sc2, gt2 = jnp.split(mod, 6, axis=-1)

    u = rms_norm(x, g_pre1) * (1.0 + sc1) + sh1
    proj = u @ w_in
    idx = np.cumsum(np.array(IN_SIZES))[:-1].tolist()
    rq, rk, rv, rg, aq, ak, av, gr, ga = jnp.split(proj, idx, axis=-1)

    rq = rotary(rq.reshape(B, L, N_RET_HEADS, RET_DK), pos)
    rk = rotary(rk.reshape(B, L, N_RET_HEADS, RET_DK), pos) * (RET_DK ** -0.5)
    rv = rv.reshape(B, L, N_RET_HEADS, RET_DV)
    o_ret, ret_state = ret_mix(rq, rk, rv)
    y_ret = (jax.nn.silu(rg) * rms_norm(o_ret).reshape(B, L, RET_V)) @ w_br_ret

    aq = aq.reshape(B, L, N_ATT_HEADS, ATT_HEAD_DIM)
    ak = ak.reshape(B, L, N_ATT_HEADS, ATT_HEAD_DIM)
    av = av.reshape(B, L, N_ATT_HEADS, ATT_HEAD_DIM)
    o_att, k_rows, v_rows = att_mix(aq, ak, av)
    y_att = o_att.reshape(B, L, ATT_W) @ w_br_att

    merged = jax.nn.sigmoid(gr) * y_ret + jax.nn.sigmoid(ga) * y_att
    x = x + gt1 * rms_norm(merged @ w_out, g_post1)

    u2 = rms_norm(x, g_pre2) * (1.0 + sc2) + sh2
    f, conv_state = conv_ffn(u2, conv_buf, w_up, conv_w, conv_b, w_down)
    x = x + gt2 * rms_norm(f, g_post2)
    return x, ret_state, k_rows, v_rows, conv_state


def setup_inputs(seed: int = 0) -> dict:
    key = jax.random.key(seed)
    ks = jax.random.split(key, 24)

    def nrm(k, shape, scale):
        return jax.random.normal(k, shape, jnp.float32) * scale

    P = min(ATT_REACH, PAST_LEN)
    L = DEPTH
    return {
        'x_prompt': nrm(ks[0], (BATCH, SEQ, D_MODEL), 1.0),
        'x_sample': nrm(ks[1], (DEC_BATCH, DEC_SEQ, D_MODEL), 1.0),
        'cache_att_k': nrm(ks[2], (L, DEC_BATCH, P, N_ATT_HEADS, ATT_HEAD_DIM), 1.0),
        'cache_att_v': nrm(ks[3], (L, DEC_BATCH, P, N_ATT_HEADS, ATT_HEAD_DIM), 1.0),
        'state_ret': nrm(ks[4], (L, DEC_BATCH, N_RET_HEADS, RET_DK, RET_DV), 1.0),
        'state_conv': nrm(ks[5], (L, DEC_BATCH, CONV_W - 1, 2 * D_FF), 1.0),
        'c_prompt': nrm(ks[6], (BATCH, D_MODEL), 1.0),
        'c_sample': nrm(ks[7], (DEC_BATCH, D_MODEL), 1.0),
        'w_ada': nrm(ks[8], (L, D_MODEL, 6 * D_MODEL), D_MODEL ** -0.5),
        'b_ada': nrm(ks[9], (L, 6 * D_MODEL), 0.01),
        'g_pre1': 1.0 + nrm(ks[10], (L, D_MODEL), 0.01),
        'w_in': nrm(ks[11], (L, D_MODEL, IN_WIDTH), D_MODEL ** -0.5),
        'rel_bias': nrm(ks[12], (L, N_ATT_HEADS, 2 * REL_CLIP + 1), 0.5),
        'w_br_ret': nrm(ks[13], (L, RET_V, D_MODEL), RET_V ** -0.5),
        'w_br_att': nrm(ks[14], (L, ATT_W, D_MODEL), ATT_W ** -0.5),
        'w_out': nrm(ks[15], (L, D_MODEL, D_MODEL), D_MODEL ** -0.5),
        'g_post1': 1.0 + nrm(ks[16], (L, D_MODEL), 0.01),
        'g_pre2': 1.0 + nrm(ks[17], (L, D_MODEL), 0.01),
        'w_up': nrm(ks[18], (L, D_MODEL, 2 * D_FF), D_MODEL ** -0.5),
        'conv_w': nrm(ks[19], (L, CONV_W, 2 * D_FF), CONV_W ** -0.5),
        'conv_b': nrm(ks[20], (L, 2 * D_FF), 0.01),
        'w_down': nrm(ks[21], (L, D_FF, D_MODEL), D_FF ** -0.5),
        'g_post2': 1.0 + nrm(ks[22], (L, D_MODEL), 0.01),
    }


def reference(x_prompt, x_sample, cache_att_k, cache_att_v, state_ret, state_conv, c_prompt, c_sample,
              w_ada, b_ada, g_pre1, w_in, rel_bias, w_br_ret, w_br_att, w_out, g_post1, g_pre2,
              w_up, conv_w, conv_b, w_down, g_post2):
    log_gamma = jnp.log(1.0 - 2.0 ** (-5.0 - jnp.arange(N_RET_HEADS, dtype=jnp.float32)))
    pos_p = jnp.arange(x_prompt.shape[1])
    pos_s = PAST_LEN + jnp.arange(x_sample.shape[1])
    conv0 = jnp.zeros((x_prompt.shape[0], CONV_W - 1, 2 * D_FF), x_prompt.dtype)
    yp, ys = x_prompt, x_sample
    kp_l, vp_l, rp_l, cp_l, ks_l, vs_l, rs_l, cs_l = [], [], [], [], [], [], [], []
    for l in range(DEPTH):
        lw = (w_ada[l], b_ada[l], g_pre1[l], w_in[l], w_br_ret[l], w_br_att[l], w_out[l], g_post1[l],
              g_pre2[l], w_up[l], conv_w[l], conv_b[l], w_down[l], g_post2[l])
        ret_p = functools.partial(retention_prompt, log_gamma=log_gamma)
        att_p = functools.partial(attention_prompt, rel_bias=rel_bias[l])
        ret_s = functools.partial(retention_sample, state=state_ret[l], log_gamma=log_gamma)
        att_s = functools.partial(attention_sample, cache_k=cache_att_k[l], cache_v=cache_att_v[l],
                                  rel_bias=rel_bias[l])
        yp, r_p, k_p, v_p, cv_p = layer_step(yp, c_prompt, pos_p, ret_p, att_p, conv0, *lw)
        ys, r_s, k_s, v_s, cv_s = layer_step(ys, c_sample, pos_s, ret_s, att_s, state_conv[l], *lw)
        kp_l.append(k_p); vp_l.append(v_p); rp_l.append(r_p); cp_l.append(cv_p)
        ks_l.append(k_s); vs_l.append(v_s); rs_l.append(r_s); cs_l.append(cv_s)
    return (yp, ys, jnp.stack(kp_l), jnp.stack(vp_l), jnp.stack(rp_l), jnp.stack(cp_l),
            jnp.stack(ks_l), jnp.stack(vs_l), jnp.stack(rs_l), jnp.stack(cs_l))
```

```python
import os
import numpy as np
from contextlib import ExitStack
import concourse.bass as bass
import concourse.mybir as mybir
from concourse.bass_utils import run_bass_kernel_spmd

F32 = mybir.dt.float32
BF16 = mybir.dt.bfloat16
U8 = mybir.dt.uint8
AF = mybir.ActivationFunctionType
ALU = mybir.AluOpType
AX = mybir.AxisListType

D = 2048
KC = 16
DFF = 5632
NFC = 44
PAST = 4096
EPS = 1e-6
NEG = -30000.0
C_RQ, C_RK, C_RV, C_RG, C_AQ, C_AK, C_AV, C_GR, C_GA = 0, 1024, 2048, 4096, 6144, 7168, 8192, 9216, 11264
ENGS = ('pe', 'act', 'dve', 'pool', 'sp')
NP = 8
NW = 4


_STOP = 99


class Prog:
    def __init__(self, nc, es):
        self.nc, self.es = nc, es
        self.q = {e: [] for e in ENGS}
        self.cnt = {e: 0 for e in ENGS}
        self.sem = {e: es.enter_context(nc.semaphore('s_' + e)) for e in ENGS}
        self.seen = {e: {} for e in ENGS}
        self.dsem = {q: [es.enter_context(nc.semaphore(f'd_{q}{i}')) for i in range(NP)] for q in ('pool', 'sp')}
        self.dcnt = {q: [0] * NP for q in ('pool', 'sp')}
        self.dnext = {'pool': 0, 'sp': 0}
        self.lastw = {}
        self.readers = {}
        self.nbank = 0
        self.dead = False
        self.pinned = set()

    def chk(self, k):
        if _STOP <= k:
            self.dead = True

    @staticmethod
    def _bx(r, w):
        rb = [k for k in r if isinstance(k, tuple) and k[0] == 'B']
        if not rb:
            return r, w
        return [k for k in r if not (isinstance(k, tuple) and k[0] == 'B')], list(w) + rb

    def _deps(self, eng, r, w):
        deps = {}

        def add(tok):
            if tok is None:
                return
            k, v = tok
            if eng == 'pe' and k == 'pe':
                return
            if deps.get(k, 0) < v:
                deps[k] = v
        for k in r:
            add(self.lastw.get(k))
        for k in w:
            add(self.lastw.get(k))
            for sk, v in self.readers.get(k, {}).items():
                add((sk, v))
        out = []
        for k, v in deps.items():
            if self.seen[eng].get(k, 0) >= v:
                continue
            self.seen[eng][k] = v
            out.append((k, v))
        return out

    def _commit(self, tok, r, w):
        for k in r:
            d = self.readers.setdefault(k, {})
            if d.get(tok[0], 0) < tok[1]:
                d[tok[0]] = tok[1]
        for k in w:
            self.lastw[k] = tok
            self.readers[k] = {}

    def op(self, eng, name, r=(), w=(), **kw):
        if self.dead:
            return None
        r, w = self._bx(r, w)
        waits = self._deps(eng, r, w)
        self.cnt[eng] += 1
        tok = (eng, self.cnt[eng])
        self.q[eng].append((waits, name, kw, 1, eng))
        self._commit(tok, r, w)
        return tok

    def mm(self, items, r=(), w=()):
        if self.dead:
            return None
        waits = self._deps('pe', r, w)
        self.cnt['pe'] += 1
        tok = ('pe', self.cnt['pe'])
        n = len(items)
        for i, (name, kw) in enumerate(items):
            self.q['pe'].append((waits if i == 0 else [], name, kw, 1 if i == n - 1 else 0, 'pe'))
        self._commit(tok, r, w)
        return tok

    def dma(self, q, out, in_, r=(), w=(), **kw):
        if self.dead:
            return None
        i = self.dnext[q]
        self.dnext[q] = (i + 1) % NP
        sk = ('d', q, i)
        waits = self._deps(q, r, w)
        prev = self.dcnt[q][i]
        if prev and self.seen[q].get(sk, 0) < prev:
            self.seen[q][sk] = prev
            waits.append((sk, prev))
        self.dcnt[q][i] = prev + 16
        tok = (sk, prev + 16)
        kw = dict(kw)
        kw['out'] = out
        kw['in_'] = in_
        self.q[q].append((waits, 'dma_start', kw, 16, sk))
        self._commit(tok, r, w)
        return tok

    def _semof(self, k):
        if isinstance(k, tuple):
            return self.dsem[k[1]][k[2]]
        return self.sem[k]

    def bank(self, avoid=(), pin=False, only=None):
        while (self.nbank % 8) in avoid or (self.nbank % 8) in self.pinned or (only is not None and (self.nbank % 8) not in only):
            self.nbank += 1
        b = self.nbank % 8
        self.nbank += 1
        if pin:
            self.pinned.add(b)
        return b

    def unpin(self, b):
        self.pinned.discard(b)

    def bank2(self, only=None):
        while (self.nbank % 2) or (self.nbank % 8) in self.pinned or ((self.nbank + 1) % 8) in self.pinned \
                or (only is not None and (self.nbank % 8) not in only):
            self.nbank += 1
        b = self.nbank % 8
        self.nbank += 2
        return b

    def run(self, final):
        nc = self.nc
        with nc.Block() as block:
            def mk(en):
                def body(e):
                    for (waits, name, kw, inc, sk) in self.q[en]:
                        for (k, v) in waits:
                            e.wait_ge(self._semof(k), v)
                        ins = getattr(e, name)(**kw)
                        if inc:
                            ins.then_inc(self._semof(sk), inc)
                    if en == 'sp':
                        for (k, v) in final:
                            e.wait_ge(self._semof(k), v)
                return body
            block.tensor(mk('pe'))
            block.scalar(mk('act'))
            block.vector(mk('dve'))
            block.gpsimd(mk('pool'))
            block.sync(mk('sp'))


def _consts(S):
    half = 64
    inv = (10000.0 ** (-np.arange(half, dtype=np.float32) / half)).astype(np.float32)
    pos = np.concatenate([np.arange(S), PAST + np.arange(16)]).astype(np.float32)
    ang = (pos[:, None] * inv[None, :]).astype(np.float32)
    rc = np.cos(ang).astype(np.float32)
    rs = np.sin(ang).astype(np.float32)
    lg = np.log(1.0 - 2.0 ** (-5.0 - np.arange(8, dtype=np.float64)))
    sc = 128.0 ** -0.5

    def tabs(L):
        m = np.arange(L)[:, None, None]
        n = np.arange(L)[None, None, :]
        h = lg[None, :, None]
        dt = np.exp(h * (np.abs(n - m) - n - 1.0)) * sc
        dk = np.exp(lg[None, :] * (L - 1.0 - np.arange(L)[:, None])) * sc
        ep = EPS * np.exp(-2.0 * lg[None, :] * (np.arange(L)[:, None] + 1.0))
        ds = np.exp(L * lg)[None, :].repeat(128, 0)
        reps = 128 // L
        dt = np.tile(dt, (reps, 1, 1))
        dk = np.tile(dk, (reps, 1))
        ep = np.tile(ep, (reps, 1))
        dtp = np.zeros((128, 8, 64), np.float32)
        dtp[:, :, :L] = dt
        return dtp, dk.astype(np.float32), ep.astype(np.float32), ds.astype(np.float32)
    dp = tabs(64)
    dsm = tabs(16)
    return dict(rot_cos=rc, rot_sin=rs, dt_p=dp[0], dk_p=dp[1], ep_p=dp[2], ds_p=dp[3],
                dt_s=dsm[0], dk_s=dsm[1], ep_s=dsm[2], ds_s=dsm[3])


def build(S):
    NT = S // 256
    KEEP = min(512, S)
    nc = bass.Bass("TRN2", target_bir_lowering=False)

    def din(name, shape):
        return nc.dram_tensor(name, list(shape), F32, kind="ExternalInput").ap()

    def dout(name, shape):
        return nc.dram_tensor(name, list(shape), F32, kind="ExternalOutput").ap()
    xp = din("xp", [S, D]); xs = din("xs", [32, D])
    ck = din("ck", [2, 512, 1024]); cv = din("cv", [2, 512, 1024])
    sret = din("sret", [2, 8, 128, 256]); sconv = din("sconv", [2, 2, 2 * DFF]); c3 = din("c3", [3, D])
    w_ada = din("w_ada", [D, 6 * D]); b_ada = din("b_ada", [6 * D]); g_pre1 = din("g_pre1", [D])
    w_in = din("w_in", [D, 13312]); rel_bias = din("rel_bias", [8, 257])
    w_br_ret = din("w_br_ret", [D, D]); w_br_att = din("w_br_att", [1024, D]); w_out = din("w_out", [D, D])
    g_post1 = din("g_post1", [D]); g_pre2 = din("g_pre2", [D]); w_up = din("w_up", [D, 2 * DFF])
    conv_w = din("conv_w", [3, 2 * DFF]); conv_b = din("conv_b", [2 * DFF]); w_down = din("w_down", [DFF, D])
    g_post2 = din("g_post2", [D])
    rot_cos = din("rot_cos", [S + 16, 64]); rot_sin = din("rot_sin", [S + 16, 64])
    cdt = {k: din(k, [128, 8, 64]) for k in ("dt_p", "dt_s")}
    csm = {k: din(k, [128, 8]) for k in ("dk_p", "ep_p", "ds_p", "dk_s", "ep_s", "ds_s")}
    yp = dout("yp", [S, D]); ys = dout("ys", [32, D])
    nkp = dout("nkp", [KEEP, 1024]); nvp = dout("nvp", [KEEP, 1024])
    nrp = dout("nrp", [8, 128, 256]); ncp = dout("ncp", [2, 2 * DFF])
    nks = dout("nks", [32, 1024]); nvs = dout("nvs", [32, 1024])
    nrs = dout("nrs", [2, 8, 128, 256]); ncs = dout("ncs", [2, 2, 2 * DFF])

    es = ExitStack()
    with es:
        P = Prog(nc, es)

        def sb(name, shape, dt=F32):
            return es.enter_context(nc.sbuf_tensor(name, list(shape), dt))
        psum = es.enter_context(nc.psum_tensor("psum", [128, 4096], F32))

        def BK(b):
            return ('B', b)

        def bankf(b, n=512):
            return psum[:, b * 512: b * 512 + n]

        def bankb(b):
            return psum[:, b * 512:(b + 1) * 512].bitcast(BF16)

        ident_f = sb("ident_f", [128, 128]); ident_b = sb("ident_b", [128, 128], BF16)
        wring = [sb(f"wr{i}", [128, 8, 512], BF16) for i in range(NW)]
        x_sb = sb("x_sb", [128, 2, 2, D])
        akT = sb("akT", [128, 8, 768], BF16); av = sb("av", [128, 6, 1024], BF16)
        S_sb = sb("S_sb", [128, 8, 256]); Sb_sb = sb("Sb_sb", [128, 8, 256], BF16)
        bias_sb = sb("bias_sb", [128, 8, 256]); cb_sb = sb("cb_sb", [128, 8])
        gg_sb = sb("gg_sb", [128, 2, D])
        gpre = sb("gpre", [128, 2, 16]); badaT = sb("badaT", [128, 96]); cT = sb("cT", [128, 16, 3])
        scT = sb("scT", [128, 16, 3], BF16)
        modT = sb("modT", [128, 96, 3])
        gm = sb("gm", [128, 2, 3, 16]); sh = sb("sh", [128, 2, 3, 16])
        cw = sb("cw", [128, 3, 88]); cbT = sb("cbT", [128, 88])
        hist = sb("hist", [128, 88, 2])
        cos_sb = sb("cos_sb", [128, 2, 64]); sin_sb = sb("sin_sb", [128, 2, 64])
        dts = {k: sb("c_" + k, [128, 8, 64]) for k in cdt}
        sms = {k: sb("c_" + k, [128, 8]) for k in csm}
        st = sb("st", [128, 64])
        ARENA = 71680
        arena = sb("arena", [128, ARENA], U8)

        def AV(off, nbytes, dt, pat=None, **kw):
            v = arena[:, off:off + nbytes]
            if dt != U8:
                v = v.bitcast(dt)
            if pat:
                v = v.rearrange(pat, **kw)
            return v

        def AK(off, nbytes):
            return [('A', i) for i in range(off // 1024, (off + nbytes - 1) // 1024 + 1)]

        screp = AV(49152 + 2048, 4096, BF16, "p (a b) -> p a b", a=16); screpk = AK(49152 + 2048, 4096)
        bc1 = AV(49152 + 6144, 2048, F32); bc1k = AK(49152 + 6144, 2048)
        bc2 = AV(49152 + 8192, 2048, F32); bc2k = AK(49152 + 8192, 2048)
        wv = {}
        for nm, apx in (("w_ada", w_ada), ("w_in", w_in), ("w_br_ret", w_br_ret), ("w_br_att", w_br_att),
                        ("w_out", w_out), ("w_up", w_up), ("w_down", w_down)):
            wv[nm] = apx.rearrange("(kc p) f -> p kc f", p=128)
        wstate = {'n': 0}
        wsc = nc.dram_tensor("wsc", [160, 128, 4096], BF16).ap()
        wsc_idx = {}

        def gemm(*a, **k):
            for _ in gemm_gen(*a, **k):
                pass

        def gemm_gen(wname, nk, col0, feat=None, tok=None, ncols=512, banks=None):
            npieces = (nk + 7) // 8
            nj = ncols // 128
            fb = [P.bank(pin=True, only=banks) for _ in range(nj)] if feat else []
            tb = [P.bank(pin=True, only=banks) for _ in tok['groups']] if tok else []
            for pi in range(npieces):
                k0 = pi * 8
                nkp = min(8, nk - k0)
                s = wstate['n'] % NW
                wstate['n'] += 1
                pkey = (wname, col0, pi)
                if wname == "w_ada":
                    P.dma('pool', wring[s][:, 0:nkp, 0:ncols], wv[wname][:, k0:k0 + nkp, col0:col0 + ncols], w=[('W', s)])
                elif pkey not in wsc_idx:
                    idx = len(wsc_idx)
                    wsc_idx[pkey] = idx
                    P.dma('pool', wring[s][:, 0:nkp, 0:ncols], wv[wname][:, k0:k0 + nkp, col0:col0 + ncols], w=[('W', s)])
                    P.dma('sp', wsc[idx, :, 0:nkp * ncols].rearrange("p (k f) -> p k f", k=nkp), wring[s][:, 0:nkp, 0:ncols],
                          r=[('W', s)], w=[('WS', idx)])
                else:
                    idx = wsc_idx[pkey]
                    P.dma('pool', wring[s][:, 0:nkp, 0:ncols], wsc[idx, :, 0:nkp * ncols].rearrange("p (k f) -> p k f", k=nkp),
                          r=[('WS', idx)], w=[('W', s)])
                if feat:
                    N = feat['N']
                    for j in range(nj):
                        items = []
                        for kl in range(nkp):
                            kc = k0 + kl
                            items.append(('matmul', dict(out=bankf(fb[j], N), lhsT=wring[s][:, kl, j * 128:(j + 1) * 128],
                                                         rhs=feat['act'][:, kc, 0:N], start=(kc == 0), stop=(kc == nk - 1))))
                        P.mm(items, r=[('W', s)] + feat['rkeys'], w=[BK(fb[j])])
                        yield
                if tok:
                    for gi, grp_ in enumerate(tok['groups']):
                        fn, M, rk = grp_[0], grp_[1], grp_[2]
                        if len(grp_) > 3:
                            rk = grp_[3](k0, nkp)
                        items = []
                        for kl in range(nkp):
                            kc = k0 + kl
                            items.append(('matmul', dict(out=psum[0:M, tb[gi] * 512: tb[gi] * 512 + ncols], lhsT=fn(kc),
                                                         rhs=wring[s][:, kl, 0:ncols], start=(kc == 0), stop=(kc == nk - 1))))
                        P.mm(items, r=[('W', s)] + rk, w=[BK(tb[gi])])
                        yield
            if feat:
                for j in range(nj):
                    feat['epi'](j, fb[j])
                    if not feat.get('hold'):
                        P.unpin(fb[j])
            if tok:
                for gi in range(len(tok['groups'])):
                    tok['epi'](gi, tb[gi])
                    P.unpin(tb[gi])

        outtoks = []
        P.op('pool', 'memset', w=['identf'], ap=ident_f[:], constant=0.0)
        P.op('pool', 'affine_select', r=['identf'], w=['identf'], out=ident_f[:], in_=ident_f[:], pattern=[[-1, 128]],
             compare_op=ALU.not_equal, fill=1.0, base=0, channel_multiplier=1)
        P.op('pool', 'tensor_copy', r=['identf'], w=['identb'], out=ident_b[:], in_=ident_f[:])
        P.op('pool', 'memset', w=['hist'], ap=hist[:], constant=0.0)
        P.op('pool', 'memset', w=['S'], ap=S_sb[:], constant=0.0)
        P.op('pool', 'memset', w=['Sb'], ap=Sb_sb[:], constant=0.0)
        nck = dict(allow_slow_non_contiguous=True)
        ldtmp = [AV(49152 + i * 512, 512, F32) for i in range(2)]
        ldk = [AK(49152, 1024)] * 2
        ldn = {'n': 0}

        def loadT(src2d, n, dst, dkeys, pat=None, **pkw):
            i = ldn['n'] % 2
            ldn['n'] += 1
            P.dma('sp', ldtmp[i][:n, :], src2d, w=ldk[i])
            b = P.bank()
            P.mm([('transpose', dict(out=psum[:, b * 512: b * 512 + n], in_=ldtmp[i][:n, :], identity=ident_f[:n, :n]))],
                 r=ldk[i] + ['identf'], w=[BK(b)])
            src = psum[:, b * 512: b * 512 + n]
            if pat:
                src = src.rearrange(pat, **pkw)
            P.op('dve', 'tensor_copy', r=[BK(b)], w=dkeys, out=dst, in_=src)

        loadT(g_pre1.rearrange("(c p) -> c p", p=128), 16, gpre[:, 0, :], ['gpre'])
        loadT(g_pre2.rearrange("(c p) -> c p", p=128), 16, gpre[:, 1, :], ['gpre'])
        loadT(b_ada.rearrange("(c p) -> c p", p=128), 96, badaT[:], ['badaT'])
        loadT(c3.rearrange("r (c p) -> (r c) p", p=128), 48, cT[:].rearrange("p c r -> p r c"), ['cT'],
              pat="p (r c) -> p r c", r=3)
        for j in range(3):
            loadT(conv_w[j].rearrange("(c p) -> c p", p=128), 88, cw[:, j, :], ['cw'])
        loadT(conv_b.rearrange("(c p) -> c p", p=128), 88, cbT[:], ['cbT'])
        for h in range(8):
            P.dma('sp', cb_sb[:, h:h + 1], rel_bias[h, 0:1].partition_broadcast(128), w=['cb'])
        for k in cdt:
            P.dma('sp', dts[k][:], cdt[k], w=['c_' + k])
        for k in csm:
            P.dma('sp', sms[k][:], csm[k], w=['c_' + k])
        P.op('dve', 'tensor_copy', r=['cb'], w=['bias'], out=bias_sb[:], in_=cb_sb[:, :].unsqueeze(2).broadcast_to([128, 8, 256]))
        for i in range(128):
            P.dma('sp', bias_sb[i:i + 1, :, i:256], rel_bias[:, 0:256 - i].unsqueeze(0), r=[], w=['bias'])
        P.op('dve', 'memset', w=['bias'], ap=bias_sb[0:64, :, 192:256], constant=NEG)
        P.chk(1)
        P.op('act', 'activation', r=['cT'], w=['scT'], out=scT[:], in_=cT[:], func=AF.Silu)

        def mod_epi(cb):
            def epi(j, b):
                ch = cb * 4 + j
                P.op('dve', 'tensor_scalar', r=[BK(b), 'badaT'], w=['modT'], out=modT[:, ch, :], in0=bankf(b, 3),
                     scalar1=badaT[:, ch:ch + 1], scalar2=None, op0=ALU.add)
            return epi

        def gg_block(cb, row, which, gpost):
            lc = (cb % 4) * 512
            P.dma('sp', bc1[:], b_ada[cb * 512:(cb + 1) * 512].partition_broadcast(128), w=bc1k)
            P.dma('sp', bc2[:], gpost[lc:lc + 512].partition_broadcast(128), w=bc2k)

            def epi(gi, b):
                P.op('dve', 'tensor_tensor', r=[BK(b)] + bc1k, w=['gg'], out=gg_sb[:, which, lc:lc + 512], in0=bankf(b),
                     in1=bc1[:], op=ALU.add)
                P.op('dve', 'tensor_tensor', r=['gg'] + bc2k, w=['gg'], out=gg_sb[:, which, lc:lc + 512],
                     in0=gg_sb[:, which, lc:lc + 512], in1=bc2[:], op=ALU.mult)
            return dict(groups=[(lambda kc: screp[:, kc, :], 128, screpk)], epi=epi)

        def set_screp(row):
            P.op('dve', 'tensor_copy', r=['scT'], w=screpk, out=screp[:, :, :],
                 in_=scT[:, :, row:row + 1].broadcast_to([128, 16, 128]))

        set_screp(0)
        for cb in range(24):
            tokspec = None
            if 8 <= cb < 12:
                tokspec = gg_block(cb, 0, 0, g_post1)
            elif 20 <= cb < 24:
                tokspec = gg_block(cb, 0, 1, g_post2)
            gemm("w_ada", 16, cb * 512, feat=dict(act=scT, N=3, rkeys=['scT'], epi=mod_epi(cb)), tok=tokspec)
        for r_ in range(3):
            for which, (c_sh, c_sc) in enumerate(((0, 16), (48, 64))):
                P.op('dve', 'scalar_tensor_tensor', r=['modT', 'gpre'], w=['gm'], out=gm[:, which, r_, :],
                     in0=modT[:, c_sc:c_sc + 16, r_], scalar=1.0, in1=gpre[:, which, :], op0=ALU.add, op1=ALU.mult)
                P.op('dve', 'tensor_copy', r=['modT'], w=['sh'], out=sh[:, which, r_, :], in_=modT[:, c_sh:c_sh + 16, r_])

        P.chk(2)
        def load_x(kind, t, xb):
            if kind == 'p':
                src = xp[t * 256:(t + 1) * 256, :].rearrange("(g p) d -> p g d", p=128)
                P.dma('sp', x_sb[:, xb, 0:2, :], src, w=[('x', xb, g) for g in range(2)])
            else:
                src = xs[t * 16:(t + 1) * 16, :].rearrange("(g p) d -> p g d", p=16)
                P.dma('sp', x_sb[:16, xb, 0:1, :], src, w=[('x', xb, 0)])

        def tile(kind, t, xb, nxt):
            prompt = (kind == 'p')
            if prompt:
                R, G, L, CPG = 128, 2, 64, 2
                row = 0
                xsrc = xp[t * 256:(t + 1) * 256, :].rearrange("(g p) d -> p g d", p=128)
                ydst = yp[t * 256:(t + 1) * 256, :].rearrange("(g p) d -> p g d", p=128)
                pos0 = t * 256
                sfx = "_p"
                qg0 = 2 * t
            else:
                R, G, L, CPG = 16, 1, 16, 1
                row = 1 + t
                xsrc = xs[t * 16:(t + 1) * 16, :].rearrange("(g p) d -> p g d", p=16)
                ydst = ys[t * 16:(t + 1) * 16, :].rearrange("(g p) d -> p g d", p=16)
                pos0 = S + 0
                sfx = "_s"
            TT = G * R
            DT, DKT, EPT, DST = dts["dt" + sfx], sms["dk" + sfx], sms["ep" + sfx], sms["ds" + sfx]
            uT = AV(0, 8192, BF16, "p (a b) -> p a b", a=16); uTk = AK(0, 8192)
            qT = AV(8192, 4096, BF16, "p (a b) -> p a b", a=8); qTk = AK(8192, 4096)
            kT = AV(12288, 4096, BF16, "p (a b) -> p a b", a=8); kTk = AK(12288, 4096)
            kd = AV(16384, 4096, BF16, "p (a b) -> p a b", a=2); kdk = AK(16384, 4096)
            qkt = AV(20480, 2048, BF16, "p (a b) -> p a b", a=2); qktk2 = [AK(20480 + i * 1024, 1024) for i in range(2)]
            mretT = AV(8192, 16384, F32, "p (a b) -> p a b", a=16); mretk = AK(8192, 16384)
            fbuf = AV(8192, 16384, F32, "p (a b) -> p a b", a=2); fbufk = AK(8192, 16384)
            vbuf = [AV(24576 + i * 2048, 2048, BF16, "p (a b) -> p a b", a=2) for i in range(2)]
            vbk = [AK(24576 + i * 2048, 2048) for i in range(2)]
            sgbuf = [AV(28672 + i * 2048, 2048, BF16, "p (a b) -> p a b", a=2) for i in range(2)]
            sgk = [AK(28672 + i * 2048, 2048) for i in range(2)]
            aqT = AV(24576, 4096, BF16, "p (a b) -> p a b", a=8); aqTk = AK(24576, 4096)
            oaT = AV(28672, 4096, BF16, "p (a b) -> p a b", a=8); oaTk = AK(28672, 4096)
            goT = AV(32768, 8192, BF16, "p (a b) -> p a b", a=16); goTk = AK(32768, 8192)
            mgT = AV(40960, 8192, BF16, "p (a b) -> p a b", a=16); mgTk = AK(40960, 8192)
            hT = AV(49152, 22528, BF16, "p (a b) -> p a b", a=44); hTk = AK(49152, 22528)
            R5 = 49152

            def norm_to_uT(which):
                xn = AV(R5, 8192, F32); xnk = AK(R5, 8192)
                junk = AV(R5 + 8192, 4096, BF16); jk = AK(R5 + 8192, 4096)
                for g in range(G):
                    P.op('act', 'activation', r=[('x', xb, g)], w=jk + [('st', 0)], out=junk[:R, :], in_=x_sb[:R, xb, g, :], func=AF.Square,
                         accum_out=st[:R, 0:1])
                    P.op('dve', 'tensor_scalar', r=[('st', 0)], w=[('st', 1)], out=st[:R, 1:2], in0=st[:R, 0:1], scalar1=1.0 / D,
                         scalar2=EPS, op0=ALU.mult, op1=ALU.add)
                    P.op('act', 'activation', r=[('st', 1)], w=[('st', 2)], out=st[:R, 2:3], in_=st[:R, 1:2], func=AF.Sqrt)
                    P.op('dve', 'reciprocal', r=[('st', 2)], w=[('st', 3)], out=st[:R, 3:4], in_=st[:R, 2:3])
                    P.op('dve', 'tensor_scalar', r=[('x', xb, g), ('st', 3)], w=xnk, out=xn[:R, :], in0=x_sb[:R, xb, g, :], scalar1=st[:R, 3:4],
                         scalar2=None, op0=ALU.mult)
                    for c4 in range(4):
                        b = P.bank()
                        items = [('transpose', dict(out=psum[:, b * 512 + j * R: b * 512 + (j + 1) * R],
                                                    in_=xn[:R, (c4 * 4 + j) * 128:(c4 * 4 + j + 1) * 128],
                                                    identity=ident_f[:R, :R])) for j in range(4)]
                        P.mm(items, r=xnk + ['identf'], w=[BK(b)])
                        for j in range(4):
                            c = c4 * 4 + j
                            if j % 2 == 0:
                                P.op('act', 'activation', r=[BK(b), 'gm', 'sh'], w=uTk, out=uT[:, c, g * R:(g + 1) * R],
                                     in_=psum[:, b * 512 + j * R: b * 512 + (j + 1) * R], func=AF.Identity,
                                     scale=gm[:, which, row, c:c + 1], bias=sh[:, which, row, c:c + 1])
                            else:
                                P.op('dve', 'tensor_scalar', r=[BK(b), 'gm', 'sh'], w=uTk, out=uT[:, c, g * R:(g + 1) * R],
                                     in0=psum[:, b * 512 + j * R: b * 512 + (j + 1) * R], scalar1=gm[:, which, row, c:c + 1],
                                     scalar2=sh[:, which, row, c:c + 1], op0=ALU.mult, op1=ALU.add)

            P.dma('sp', cos_sb[:R, 0:G, :], rot_cos[pos0:pos0 + TT, :].rearrange("(g p) j -> p g j", p=R), w=['cos'])
            P.dma('sp', sin_sb[:R, 0:G, :], rot_sin[pos0:pos0 + TT, :].rearrange("(g p) j -> p g j", p=R), w=['sin'])
            if nxt is not None:
                load_x(nxt[0], nxt[1], 1 - xb)
            norm_to_uT(0)

            P.chk(3)

            def ugroups():
                return [((lambda kc, g=g: uT[:, kc, g * R:(g + 1) * R]), R, uTk) for g in range(G)]

            rot_deferred = []

            def rot_epi(isk, cb):
                def epi(g, b):
                    qf = AV(R5 + (g % 2) * 2048, 2048, F32); qfk = AK(R5 + (g % 2) * 2048, 2048)
                    bi = ((2 if isk else 0) + cb) * G + g
                    qkb = AV(32768 + bi * 1024, 1024, BF16); qktk = AK(32768 + bi * 1024, 1024)
                    t1 = AV(R5 + 4096, 1024, F32); t2 = AV(R5 + 5120, 1024, F32); tk = AK(R5 + 4096, 2048)
                    P.op('act', 'activation', r=[BK(b)], w=qfk, out=qf[:R, :], in_=bankf(b)[:R, :], func=AF.Copy)
                    q3 = qf[:R, :].rearrange("p (h d) -> p h d", h=4)
                    x1, x2 = q3[:, :, 0:64], q3[:, :, 64:128]
                    cosb = cos_sb[:R, g:g + 1, :].broadcast_to([R, 4, 64])
                    sinb = sin_sb[:R, g:g + 1, :].broadcast_to([R, 4, 64])
                    t13 = t1[:R, :].rearrange("p (h d) -> p h d", h=4)
                    t23 = t2[:R, :].rearrange("p (h d) -> p h d", h=4)
                    o3 = qkb[:R, :].rearrange("p (h d) -> p h d", h=4)
                    P.op('dve', 'tensor_tensor', r=qfk + ['cos'], w=tk, out=t13, in0=x1, in1=cosb, op=ALU.mult)
                    P.op('dve', 'tensor_tensor', r=qfk + ['sin'], w=tk, out=t23, in0=x2, in1=sinb, op=ALU.mult)
                    P.op('dve', 'tensor_tensor', r=tk, w=qktk, out=o3[:, :, 0:64], in0=t13, in1=t23, op=ALU.subtract)
                    P.op('dve', 'tensor_tensor', r=qfk + ['sin'], w=tk, out=t13, in0=x1, in1=sinb, op=ALU.mult)
                    P.op('dve', 'tensor_tensor', r=qfk + ['cos'], w=tk, out=t23, in0=x2, in1=cosb, op=ALU.mult)
                    P.op('dve', 'tensor_tensor', r=tk, w=qktk, out=o3[:, :, 64:128], in0=t13, in1=t23, op=ALU.add)
                    dstT, dk_ = (kT, kTk) if isk else (qT, qTk)

                    def later():
                        b2 = P.bank()
                        bb = bankb(b2)
                        items = [('transpose', dict(out=bb[:, j * R:(j + 1) * R], in_=qkb[:R, j * 128:(j + 1) * 128],
                                                    identity=ident_b[:R, :R])) for j in range(4)]
                        P.mm(items, r=qktk + ['identb'], w=[BK(b2)])
                        P.op('act', 'activation', r=[BK(b2)], w=dk_, out=dstT[:, cb * 4:(cb + 1) * 4, g * R:(g + 1) * R],
                             in_=bb[:, 0:4 * R].rearrange("p (j r) -> p j r", j=4), func=AF.Copy)
                    rot_deferred.append(later)
                    if isk:
                        P.op('dve', 'tensor_tensor', r=qktk + ['c_dk' + sfx], w=kdk,
                             out=kd[:R, g, cb * 512:(cb + 1) * 512].rearrange("p (h d) -> p h d", h=4), in0=o3,
                             in1=DKT[:R, cb * 4:(cb + 1) * 4].unsqueeze(2).broadcast_to([R, 4, 128]), op=ALU.mult)
                return epi
            for cb in range(2):
                gemm("w_in", 16, C_RQ + cb * 512, tok=dict(groups=ugroups(), epi=rot_epi(False, cb)))
            for cb in range(2):
                gemm("w_in", 16, C_RK + cb * 512, tok=dict(groups=ugroups(), epi=rot_epi(True, cb)))

            P.chk(4)
            def vg_gens(hp):
                vb, sgb = vbuf[hp % 2], sgbuf[hp % 2]
                vk, sk_ = vbk[hp % 2], sgk[hp % 2]

                def v_epi(g, b):
                    P.op('act', 'activation', r=[BK(b)], w=vk, out=vb[:R, g, :], in_=bankf(b)[:R, :], func=AF.Copy)

                def g_epi(g, b):
                    P.op('act', 'activation', r=[BK(b)], w=sk_, out=sgb[:R, g, :], in_=bankf(b)[:R, :], func=AF.Silu)
                yield from gemm_gen("w_in", 16, C_RV + hp * 512, tok=dict(groups=ugroups(), epi=v_epi))
                yield from gemm_gen("w_in", 16, C_RG + hp * 512, tok=dict(groups=ugroups(), epi=g_epi))

            def advance(gen, n):
                if gen is None:
                    return
                for _ in range(n):
                    try:
                        next(gen)
                    except StopIteration:
                        return

            advance(vg_gens(0), 10 ** 6)
            for fn_ in rot_deferred:
                fn_()
            for hp in range(4):
                vb, sgb = vbuf[hp % 2], sgbuf[hp % 2]
                vk, sk_ = vbk[hp % 2], sgk[hp % 2]
                filler = vg_gens(hp + 1) if hp < 3 else None
                sTd = AV(R5 + 8192, 256, BF16); sTdk = AK(R5 + 8192, 256)
                gtok = AV(R5 + 12288, 1024, BF16); gtokk = AK(R5 + 12288, 1024)
                for g in range(G):
                    bo = P.bank(pin=True)
                    for cc in range(CPG):
                        off = cc * L
                        c0 = g * R + off
                        bkvs = {}
                        for hh in range(2):
                            h = hp * 2 + hh
                            bs = P.bank()
                            P.mm([('matmul', dict(out=psum[off:off + L, bs * 512: bs * 512 + L], lhsT=kT[:, h, c0:c0 + L],
                                                  rhs=qT[:, h, c0:c0 + L], start=True, stop=True))], r=kTk + qTk, w=[BK(bs)])
                            P.op('dve', 'tensor_tensor', r=[BK(bs), 'c_dt' + sfx], w=sTdk, out=sTd[off:off + L, hh * 64: hh * 64 + L],
                                 in0=psum[off:off + L, bs * 512: bs * 512 + L], in1=DT[off:off + L, h, 0:L], op=ALU.mult)
                        for hh in range(2):
                            h = hp * 2 + hh
                            bkv = P.bank(pin=True)
                            bkvs[hh] = bkv
                            P.mm([('matmul', dict(out=psum[:, bkv * 512: bkv * 512 + 256], lhsT=kd[off:off + L, g, h * 128:(h + 1) * 128],
                                                  rhs=vb[off:off + L, g, hh * 256:(hh + 1) * 256], start=True, stop=True))],
                                 r=kdk + vk, w=[BK(bkv)])
                        advance(filler, 2)
                        for hh in range(2):
                            h = hp * 2 + hh
                            bkv = bkvs[hh]
                            P.mm([('matmul', dict(out=psum[off:off + L, bo * 512 + hh * 256: bo * 512 + (hh + 1) * 256],
                                                  lhsT=sTd[off:off + L, hh * 64: hh * 64 + L], rhs=vb[off:off + L, g, hh * 256:(hh + 1) * 256],
                                                  start=True, stop=False)),
                                  ('matmul', dict(out=psum[off:off + L, bo * 512 + hh * 256: bo * 512 + (hh + 1) * 256],
                                                  lhsT=qT[:, h, c0:c0 + L], rhs=Sb_sb[:, h, :], start=False, stop=True))],
                                 r=sTdk + vk + qTk + [('Sb', h)], w=[BK(bo)])
                            P.op('dve', 'scalar_tensor_tensor', r=[BK(bkv), ('S', h), 'c_ds' + sfx], w=[('S', h)], out=S_sb[:, h, :],
                                 in0=S_sb[:, h, :], scalar=DST[:, h:h + 1], in1=psum[:, bkv * 512: bkv * 512 + 256],
                                 op0=ALU.mult, op1=ALU.add)
                            P.unpin(bkv)
                            P.op('act', 'activation', r=[('S', h)], w=[('Sb', h)], out=Sb_sb[:, h, :], in_=S_sb[:, h, :], func=AF.Copy)
                    junk = AV(R5 + 16384, 512, BF16); jk = AK(R5 + 16384, 512)
                    for hh in range(2):
                        h = hp * 2 + hh
                        P.op('act', 'activation', r=[BK(bo)], w=jk + [('st', 8 + hh)], out=junk[:R, 0:256],
                             in_=psum[:R, bo * 512 + hh * 256: bo * 512 + (hh + 1) * 256], func=AF.Square, accum_out=st[:R, 8 + hh: 9 + hh])
                        P.op('dve', 'scalar_tensor_tensor', r=[('st', 8 + hh), 'c_ep' + sfx], w=[('st', 10 + hh)], out=st[:R, 10 + hh: 11 + hh],
                             in0=st[:R, 8 + hh: 9 + hh], scalar=1.0 / 256, in1=EPT[:R, h:h + 1], op0=ALU.mult, op1=ALU.add)
                        P.op('act', 'activation', r=[('st', 10 + hh)], w=[('st', 12 + hh)], out=st[:R, 12 + hh: 13 + hh],
                             in_=st[:R, 10 + hh: 11 + hh], func=AF.Sqrt)
                        P.op('dve', 'reciprocal', r=[('st', 12 + hh)], w=[('st', 14 + hh)], out=st[:R, 14 + hh: 15 + hh],
                             in_=st[:R, 12 + hh: 13 + hh])
                        P.op('dve', 'scalar_tensor_tensor', r=[BK(bo), ('st', 14 + hh)] + sk_, w=gtokk,
                             out=gtok[:R, hh * 256:(hh + 1) * 256], in0=psum[:R, bo * 512 + hh * 256: bo * 512 + (hh + 1) * 256],
                             scalar=st[:R, 14 + hh: 15 + hh], in1=sgb[:R, g, hh * 256:(hh + 1) * 256], op0=ALU.mult, op1=ALU.mult)
                    P.unpin(bo)
                    b2 = P.bank()
                    bb = bankb(b2)
                    items = [('transpose', dict(out=bb[:, j * R:(j + 1) * R], in_=gtok[:R, j * 128:(j + 1) * 128],
                                                identity=ident_b[:R, :R])) for j in range(4)]
                    P.mm(items, r=gtokk + ['identb'], w=[BK(b2)])
                    P.op('act', 'activation', r=[BK(b2)], w=goTk, out=goT[:, hp * 4:(hp + 1) * 4, g * R:(g + 1) * R],
                         in_=bb[:, 0:4 * R].rearrange("p (j r) -> p j r", j=4), func=AF.Copy)
                advance(filler, 10 ** 6)

            P.chk(5)
            sgt = [AV(R5 + i * 1024, 1024, F32) for i in range(2)]; sgtk = [AK(R5 + i * 1024, 1024) for i in range(2)]
            sgtA = [AV(40960 + i * 1024, 1024, F32) for i in range(2)]; sgtAk = [AK(40960 + i * 1024, 1024) for i in range(2)]
            FB = {4, 5, 6, 7}
            AB = {0, 1, 2, 3}

            def d2_gen():
                for fb_ in range(8):
                    ybanks = {}

                    def y_epi(j, b, ybanks=ybanks):
                        ybanks[j] = b

                    def gr_epi(j, b, ybanks=ybanks, fb_=fb_):
                        c = fb_ * 2 + j
                        P.op('act', 'activation', r=[BK(b)], w=sgtAk[j % 2], out=sgtA[j % 2][:, 0:TT], in_=bankf(b, TT), func=AF.Sigmoid)
                        P.op('dve', 'tensor_tensor', r=[BK(ybanks[j])] + sgtAk[j % 2], w=mretk, out=mretT[:, c, 0:TT],
                             in0=bankf(ybanks[j], TT), in1=sgtA[j % 2][:, 0:TT], op=ALU.mult)
                        P.unpin(ybanks[j])
                    yield from gemm_gen("w_br_ret", 16, fb_ * 256, feat=dict(act=goT, N=TT, rkeys=goTk, epi=y_epi, hold=True),
                                        ncols=256, banks=FB)
                    yield from gemm_gen("w_in", 16, C_GR + fb_ * 256, feat=dict(act=uT, N=TT, rkeys=uTk, epi=gr_epi),
                                        ncols=256, banks=FB)

            P.chk(6)
            if prompt:
                slots = {}
                for g in range(G):
                    slots[g] = (qg0 + g) % 6
                keep_t = (t * 256 >= S - KEEP)
            else:
                slots = {0: 4}
                keep_t = True
                P.dma('pool', av[:, 0:4, :], cv[t].rearrange("(g p) f -> p g f", p=128), w=['av'])
                for gk in range(4):
                    ktmp = AV(R5, 2048, BF16); ktk = AK(R5, 2048)
                    P.dma('pool', ktmp[:, :], ck[t, gk * 128:(gk + 1) * 128, :], w=ktk)
                    for h4 in range(2):
                        b2 = P.bank()
                        bb = bankb(b2)
                        items = [('transpose', dict(out=bb[:, j * 128:(j + 1) * 128], in_=ktmp[:, (h4 * 4 + j) * 128:(h4 * 4 + j + 1) * 128],
                                                    identity=ident_b[:, :])) for j in range(4)]
                        P.mm(items, r=ktk + ['identb'], w=[BK(b2)])
                        P.op('act', 'activation', r=[BK(b2)], w=['akT'], out=akT[:, h4 * 4:(h4 + 1) * 4, gk * 128:(gk + 1) * 128],
                             in_=bb[:, 0:512].rearrange("p (j r) -> p j r", j=4), func=AF.Copy)

            def aq_epi(cb):
                def epi(j, b):
                    P.op('act', 'activation', r=[BK(b)], w=aqTk, out=aqT[:, cb * 4 + j, 0:TT], in_=bankf(b, TT), func=AF.Identity,
                         scale=float(128.0 ** -0.5))
                return epi

            def ak_epi(cb):
                def epi(j, b):
                    for g in range(G):
                        P.op('act', 'activation', r=[BK(b)], w=['akT'], out=akT[:, cb * 4 + j, slots[g] * 128: slots[g] * 128 + R],
                             in_=psum[:, b * 512 + g * R: b * 512 + (g + 1) * R], func=AF.Copy)
                return epi

            def kvout_epi(cb, dstp, dsts_):
                def epi(g, b):
                    tmp = AV(R5 + 4096 + (g % 2) * 2048, 2048, F32); tmk = AK(R5 + 4096 + (g % 2) * 2048, 2048)
                    P.op('dve', 'tensor_copy', r=[BK(b)], w=tmk, out=tmp[:R, :], in_=bankf(b)[:R, :])
                    if prompt:
                        r0 = t * 256 - (S - KEEP) + g * 128
                        dst = dstp[r0:r0 + 128, cb * 512:(cb + 1) * 512]
                    else:
                        dst = dsts_[t * 16:(t + 1) * 16, cb * 512:(cb + 1) * 512]
                    outtoks.append(P.dma('sp', dst, tmp[:R, :], r=tmk))
                return epi

            def av_epi(cb):
                kv = kvout_epi(cb, nvp, nvs)

                def epi(g, b):
                    P.op('act', 'activation', r=[BK(b)], w=['av'], out=av[:R, slots[g], cb * 512:(cb + 1) * 512], in_=bankf(b)[:R, :],
                         func=AF.Copy)
                    if keep_t:
                        kv(g, b)
                return epi
            for cb in range(2):
                gemm("w_in", 16, C_AQ + cb * 512, feat=dict(act=uT, N=TT, rkeys=uTk, epi=aq_epi(cb)))
            for cb in range(2):
                gemm("w_in", 16, C_AK + cb * 512, feat=dict(act=uT, N=TT, rkeys=uTk, epi=ak_epi(cb)),
                     tok=(dict(groups=ugroups(), epi=kvout_epi(cb, nkp, nks)) if keep_t else None))
            for cb in range(2):
                gemm("w_in", 16, C_AV + cb * 512, tok=dict(groups=ugroups(), epi=av_epi(cb)))

            P.chk(7)
            oats = [AV(R5 + i * 2048, 2048, BF16) for i in range(2)]
            oatks = [AK(R5 + i * 2048, 2048) for i in range(2)]
            items_gh = [(g, h) for g in range(G) for h in range(8)]
            stt_ = {}

            def kgs_of(g):
                if prompt:
                    qg = qg0 + g
                    return [(i, (qg - 4 + i) % 6, 128) for i in range(5) if qg - 4 + i >= 0], 640
                return [(i, i, 128) for i in range(4)] + [(4, 4, 16)], 528

            def stA(n):
                g, h = items_gh[n]
                kgs, jmax = kgs_of(g)
                c0 = kgs[0][0] * 128
                pb = n % 3
                sc_ = 16 + 4 * pb
                tt = AV(R5 + 4096 + pb * 2560, 2560, F32); ttk = AK(R5 + 4096 + pb * 2560, 2560)
                pp = AV(R5 + 12288 + pb * 1280, 1280, BF16); ppk = AK(R5 + 12288 + pb * 1280, 1280)
                b = P.bank2(only=AB)
                items = [('matmul', dict(out=psum[:R, b * 512 + i * 128: b * 512 + i * 128 + n_], lhsT=aqT[:, h, g * R:(g + 1) * R],
                                         rhs=akT[:, h, sl * 128: sl * 128 + n_], start=True, stop=True)) for (i, sl, n_) in kgs]
                P.mm(items, r=aqTk + ['akT'], w=[BK(b), BK(b + 1)])
                sv = psum[:R, b * 512: b * 512 + 640]
                if c0 < 384:
                    P.op('dve', 'tensor_scalar', r=[BK(b), 'cb'], w=ttk, out=tt[:R, c0:384], in0=sv[:, c0:384],
                         scalar1=cb_sb[:R, h:h + 1], scalar2=None, op0=ALU.add)
                P.op('dve', 'tensor_tensor', r=[BK(b), BK(b + 1), 'bias'], w=ttk, out=tt[:R, 384:jmax], in0=sv[:, 384:jmax],
                     in1=bias_sb[:R, h, 0:jmax - 384], op=ALU.add)
                if prompt and c0 == 0:
                    P.op('dve', 'memset', w=ttk, ap=tt[64:128, 0:64], constant=NEG)
                P.op('dve', 'reduce_max', r=ttk, w=[('st', sc_)], out=st[:R, sc_:sc_ + 1], in_=tt[:R, c0:jmax], axis=AX.X)
                P.op('dve', 'tensor_scalar', r=[('st', sc_)], w=[('st', sc_ + 1)], out=st[:R, sc_ + 1:sc_ + 2], in0=st[:R, sc_:sc_ + 1],
                     scalar1=-1.0, scalar2=None, op0=ALU.mult)
                P.op('act', 'activation', r=ttk + [('st', sc_ + 1)], w=ppk + [('st', sc_ + 2)], out=pp[:R, c0:jmax], in_=tt[:R, c0:jmax],
                     func=AF.Exp, bias=st[:R, sc_ + 1:sc_ + 2], scale=1.0, accum_out=st[:R, sc_ + 2:sc_ + 3])
                P.op('dve', 'reciprocal', r=[('st', sc_ + 2)], w=[('st', sc_ + 3)], out=st[:R, sc_ + 3:sc_ + 4], in_=st[:R, sc_ + 2:sc_ + 3])
                stt_[n] = (kgs, pp, ppk, sc_)

            def stB(n):
                g, h = items_gh[n]
                kgs, pp, ppk, sc_ = stt_[n]
                pb = n % 2
                pT = AV(R5 + 16384 + pb * 2048, 1280, BF16, "p (a b) -> p a b", a=5); pTk = AK(R5 + 16384 + pb * 2048, 1280)
                b2 = P.bank(only=AB)
                bb = bankb(b2)
                items = [('transpose', dict(out=bb[:n_, i * R:(i + 1) * R], in_=pp[:R, i * 128: i * 128 + n_],
                                            identity=ident_b[:R, :R])) for (i, sl, n_) in kgs]
                P.mm(items, r=ppk + ['identb'], w=[BK(b2)])
                i0 = kgs[0][0]
                P.op('act', 'activation', r=[BK(b2)], w=pTk, out=pT[:, i0:5, 0:R],
                     in_=bb[:, i0 * R: 5 * R].rearrange("p (a r) -> p a r", r=R), func=AF.Copy)
                stt_[n] = (kgs, pT, pTk, sc_)

            def stC(n):
                g, h = items_gh[n]
                kgs, pT, pTk, sc_ = stt_[n]
                oat, oatk = oats[g % 2], oatks[g % 2]
                b3 = P.bank(only=AB)
                nk_ = len(kgs)
                items = [('matmul', dict(out=psum[:R, b3 * 512: b3 * 512 + 128], lhsT=pT[:n_, i, 0:R],
                                         rhs=av[:n_, sl, h * 128:(h + 1) * 128], start=(ii == 0), stop=(ii == nk_ - 1)))
                         for ii, (i, sl, n_) in enumerate(kgs)]
                P.mm(items, r=pTk + ['av'], w=[BK(b3)])
                P.op('act', 'activation', r=[BK(b3), ('st', sc_ + 3)], w=oatk, out=oat[:R, h * 128:(h + 1) * 128],
                     in_=psum[:R, b3 * 512: b3 * 512 + 128], func=AF.Identity, scale=st[:R, sc_ + 3:sc_ + 4])
                if h == 7:
                    for h4 in range(2):
                        b2 = P.bank(only=AB)
                        bb = bankb(b2)
                        items = [('transpose', dict(out=bb[:, j * R:(j + 1) * R], in_=oat[:R, (h4 * 4 + j) * 128:(h4 * 4 + j + 1) * 128],
                                                    identity=ident_b[:R, :R])) for j in range(4)]
                        P.mm(items, r=oatk + ['identb'], w=[BK(b2)])
                        P.op('act', 'activation', r=[BK(b2)], w=oaTk, out=oaT[:, h4 * 4:(h4 + 1) * 4, g * R:(g + 1) * R],
                             in_=bb[:, 0:4 * R].rearrange("p (j r) -> p j r", j=4), func=AF.Copy)

            NI = len(items_gh)
            d2 = d2_gen()
            for i in range(NI + 3):
                if i >= 3:
                    stC(i - 3)
                if 2 <= i < NI + 2:
                    stB(i - 2)
                if i < NI:
                    stA(i)
                advance(d2, 4)
            advance(d2, 10 ** 6)

            P.chk(8)
            tmp2 = [AV(R5 + 2048 + i * 1024, 1024, F32) for i in range(2)]; tmp2k = [AK(R5 + 2048 + i * 1024, 1024) for i in range(2)]
            for fb_ in range(4):
                ybanks = {}

                def y_epi2(j, b, ybanks=ybanks):
                    ybanks[j] = b

                def ga_epi(j, b, ybanks=ybanks, fb_=fb_):
                    c = fb_ * 4 + j
                    P.op('act', 'activation', r=[BK(b)], w=sgtk[j % 2], out=sgt[j % 2][:, 0:TT], in_=bankf(b, TT), func=AF.Sigmoid)
                    P.op('dve', 'tensor_tensor', r=[BK(ybanks[j])] + sgtk[j % 2], w=tmp2k[j % 2], out=tmp2[j % 2][:, 0:TT],
                         in0=bankf(ybanks[j], TT), in1=sgt[j % 2][:, 0:TT], op=ALU.mult)
                    P.unpin(ybanks[j])
                    P.op('dve', 'tensor_tensor', r=tmp2k[j % 2] + mretk, w=mgTk, out=mgT[:, c, 0:TT], in0=tmp2[j % 2][:, 0:TT],
                         in1=mretT[:, c, 0:TT], op=ALU.add)
                gemm("w_br_att", 8, fb_ * 512, feat=dict(act=oaT, N=TT, rkeys=oaTk, epi=y_epi2, hold=True))
                gemm("w_in", 16, C_GA + fb_ * 512, feat=dict(act=uT, N=TT, rkeys=uTk, epi=ga_epi))

            P.chk(9)
            def resid_phase(wname, nk, act, actk, which, buf, bufk, junk_off, rkfn=None):
                junk = AV(junk_off, 1024, BF16); jk = AK(junk_off, 1024)
                if rkfn is None:
                    groups = [((lambda kc, g=g: act[:, kc, g * R:(g + 1) * R]), R, actk) for g in range(G)]
                else:
                    groups = [((lambda kc, g=g: act[:, kc, g * R:(g + 1) * R]), R, actk, rkfn) for g in range(G)]

                def mk_epi(cb):
                    def epi(g, b):
                        col = 40 + g * 4 + cb
                        P.op('act', 'activation', r=[BK(b)], w=jk + [('st', col)], out=junk[:R, :], in_=bankf(b)[:R, :], func=AF.Square,
                             accum_out=st[:R, col:col + 1])
                        P.op('dve', 'tensor_copy', r=[BK(b)], w=bufk, out=buf[:R, g, cb * 512:(cb + 1) * 512], in_=bankf(b)[:R, :])
                    return epi
                for cb in range(4):
                    gemm(wname, nk, cb * 512, tok=dict(groups=groups, epi=mk_epi(cb)))
                for g in range(G):
                    c0_ = 40 + g * 4
                    P.op('dve', 'reduce_sum', r=[('st', c0_ + i) for i in range(4)], w=[('st', 48 + g)], out=st[:R, 48 + g: 49 + g],
                         in_=st[:R, c0_: c0_ + 4], axis=AX.X)
                    P.op('dve', 'tensor_scalar', r=[('st', 48 + g)], w=[('st', 50 + g)], out=st[:R, 50 + g: 51 + g], in0=st[:R, 48 + g: 49 + g],
                         scalar1=1.0 / D, scalar2=EPS, op0=ALU.mult, op1=ALU.add)
                    P.op('act', 'activation', r=[('st', 50 + g)], w=[('st', 52 + g)], out=st[:R, 52 + g: 53 + g], in_=st[:R, 50 + g: 51 + g],
                         func=AF.Sqrt)
                    P.op('dve', 'reciprocal', r=[('st', 52 + g)], w=[('st', 54 + g)], out=st[:R, 54 + g: 55 + g], in_=st[:R, 52 + g: 53 + g])
                    P.op('dve', 'scalar_tensor_tensor', r=bufk + [('st', 54 + g), 'gg'], w=bufk, out=buf[:R, g, :], in0=buf[:R, g, :],
                         scalar=st[:R, 54 + g: 55 + g], in1=gg_sb[:R, which, :], op0=ALU.mult, op1=ALU.mult)
                    P.op('dve', 'tensor_tensor', r=bufk + [('x', xb, g)], w=[('x', xb, g)], out=x_sb[:R, xb, g, :], in0=x_sb[:R, xb, g, :], in1=buf[:R, g, :],
                         op=ALU.add)

            h1buf = AV(R5, 16384, F32, "p (a b) -> p a b", a=2); h1k = AK(R5, 16384)
            resid_phase("w_out", 16, mgT, mgTk, 0, h1buf, h1k, R5 + 16384)
            norm_to_uT(1)

            P.chk(10)
            GO = 12288
            upb = [AV(GO + i * 4608, 4 * (TT + 2) * 4, F32, "p (c n) -> p c n", c=4) for i in range(2)]
            upk = [AK(GO, 4608), AK(GO + 4608, 4608)]
            hvbs = [AV(GO + 9216 + i * 4096, 4 * TT * 4, F32, "p (c n) -> p c n", c=4) for i in range(2)]
            hvks = [[AK(GO + 9216 + i * 4096 + j * 1024, 1024) for j in range(4)] for i in range(2)]
            hgbs = [AV(GO + 17408 + i * 4096, 4 * TT * 4, F32, "p (c n) -> p c n", c=4) for i in range(2)]
            hgks = [[AK(GO + 17408 + i * 4096 + j * 1024, 1024) for j in range(4)] for i in range(2)]
            gtb = AV(GO + 25600, 4 * TT * 4, F32, "p (c n) -> p c n", c=4); gtk = AK(GO + 25600, 4096)
            if not prompt:
                for jj in range(2):
                    loadT(sconv[t, jj].rearrange("(c p) -> c p", p=128), 88, hist[:, :, jj], ['hist'])
            last_tile = (prompt and t == NT - 1) or (not prompt)

            def conv_half(isg, blk):
                ub, uk = upb[isg], upk[isg]
                hb = (hgbs if isg else hvbs)[blk % 2]
                hks = (hgks if isg else hvks)[blk % 2]
                cbase = (NFC if isg else 0) + blk * 4
                P.op('dve', 'tensor_copy', r=['hist'], w=uk, out=ub[:, :, 0:2], in_=hist[:, cbase:cbase + 4, :])

                def epi(j, b):
                    P.op('act', 'activation', r=[BK(b)], w=uk, out=ub[:, j, 2:2 + TT], in_=bankf(b, TT), func=AF.Copy)
                    if j < 3:
                        return
                    for jj in range(4):
                        c = cbase + jj
                        hk = hks[jj]
                        P.op('act', 'activation', r=uk + ['cw', 'cbT'], w=hk, out=hb[:, jj, 0:TT], in_=ub[:, jj, 2:2 + TT], func=AF.Identity,
                             scale=cw[:, 2, c:c + 1], bias=cbT[:, c:c + 1])
                        P.op('dve', 'scalar_tensor_tensor', r=uk + hk + ['cw'], w=hk, out=hb[:, jj, 0:TT], in0=ub[:, jj, 1:1 + TT],
                             scalar=cw[:, 1, c:c + 1], in1=hb[:, jj, 0:TT], op0=ALU.mult, op1=ALU.add)
                        P.op('dve', 'scalar_tensor_tensor', r=uk + hk + ['cw'], w=hk, out=hb[:, jj, 0:TT], in0=ub[:, jj, 0:TT],
                             scalar=cw[:, 0, c:c + 1], in1=hb[:, jj, 0:TT], op0=ALU.mult, op1=ALU.add)
                    P.op('dve', 'tensor_copy', r=uk, w=['hist'], out=hist[:, cbase:cbase + 4, :], in_=ub[:, :, TT:TT + 2])
                return epi

            def cstate_tok(col0):
                if not last_tile:
                    return None
                if prompt:
                    grp = [((lambda kc: uT[:, kc, TT - 2:TT]), 2, uTk)]
                else:
                    grp = [((lambda kc: uT[:, kc, 0:16]), 16, uTk)]

                def epi(gi, b):
                    tmp = AV(8192 + ((col0 // 512) % 2) * 2048, 2048, F32); tmk = AK(8192 + ((col0 // 512) % 2) * 2048, 2048)
                    if prompt:
                        P.op('dve', 'tensor_copy', r=[BK(b)], w=tmk, out=tmp[0:2, :], in_=bankf(b)[0:2, :])
                        outtoks.append(P.dma('sp', ncp[:, col0:col0 + 512], tmp[0:2, :], r=tmk))
                    else:
                        P.op('dve', 'tensor_copy', r=[BK(b)], w=tmk, out=tmp[0:16, :], in_=bankf(b)[0:16, :])
                        outtoks.append(P.dma('sp', ncs[t, :, col0:col0 + 512], tmp[14:16, :], r=tmk))
                return dict(groups=grp, epi=epi)

            def gelu_mul(blk):
                hgb, hvb = hgbs[blk % 2], hvbs[blk % 2]
                hgk = sum(hgks[blk % 2], []); hvk = sum(hvks[blk % 2], [])
                P.op('act', 'activation', r=hgk, w=gtk, out=gtb[:, :, 0:TT], in_=hgb[:, :, 0:TT], func=AF.Square)
                P.op('dve', 'tensor_scalar', r=gtk, w=gtk, out=gtb[:, :, 0:TT], in0=gtb[:, :, 0:TT], scalar1=0.044715, scalar2=1.0,
                     op0=ALU.mult, op1=ALU.add)
                P.op('dve', 'tensor_tensor', r=gtk + hgk, w=gtk, out=gtb[:, :, 0:TT], in0=gtb[:, :, 0:TT], in1=hgb[:, :, 0:TT], op=ALU.mult)
                P.op('act', 'activation', r=gtk, w=gtk, out=gtb[:, :, 0:TT], in_=gtb[:, :, 0:TT], func=AF.Sigmoid, scale=1.5957691216057308)
                P.op('dve', 'tensor_tensor', r=gtk + hgk, w=gtk, out=gtb[:, :, 0:TT], in0=gtb[:, :, 0:TT], in1=hgb[:, :, 0:TT], op=ALU.mult)
                P.op('dve', 'tensor_tensor', r=gtk + hvk, w=AK(49152 + blk * 2048, 2048), out=hT[:, blk * 4:(blk + 1) * 4, 0:TT],
                     in0=gtb[:, :, 0:TT], in1=hvb[:, :, 0:TT], op=ALU.mult)

            for blk in range(11):
                gemm("w_up", 16, blk * 512, feat=dict(act=uT, N=TT, rkeys=uTk, epi=conv_half(0, blk)), tok=cstate_tok(blk * 512))
                gemm("w_up", 16, DFF + blk * 512, feat=dict(act=uT, N=TT, rkeys=uTk, epi=conv_half(1, blk)),
                     tok=cstate_tok(DFF + blk * 512))
                if blk >= 1:
                    gelu_mul(blk - 1)
            gelu_mul(10)

            P.chk(11)
            resid_phase("w_down", NFC, hT, hTk, 1, fbuf, fbufk, 24576, rkfn=lambda k0, nkp: AK(49152 + k0 * 512, nkp * 512))
            outtoks.append(P.dma('sp', ydst, x_sb[:R, xb, 0:G, :], r=[('x', xb, g) for g in range(G)]))

        seq = [('p', t) for t in range(NT)] + [('s', 0), ('s', 1)]
        load_x('p', 0, 0)
        for t in range(NT):
            tile('p', t, t % 2, seq[t + 1])
        outtoks.append(P.dma('sp', nrp.rearrange("h d e -> d h e"), S_sb[:], r=[('S', h) for h in range(8)]))
        for b_ in range(2):
            set_screp(1 + b_)
            for cb in (8, 9, 10, 11):
                gemm("w_ada", 16, cb * 512, tok=gg_block(cb, 1 + b_, 0, g_post1))
            for cb in (20, 21, 22, 23):
                gemm("w_ada", 16, cb * 512, tok=gg_block(cb, 1 + b_, 1, g_post2))
            P.dma('sp', S_sb[:], sret[b_].rearrange("h d e -> d h e"), w=[('S', h) for h in range(8)])
            for h in range(8):
                P.op('act', 'activation', r=[('S', h)], w=[('Sb', h)], out=Sb_sb[:, h, :], in_=S_sb[:, h, :], func=AF.Copy)
            tile('s', b_, (NT + b_) % 2, seq[NT + b_ + 1] if NT + b_ + 1 < len(seq) else None)
            outtoks.append(P.dma('sp', nrs[b_].rearrange("h d e -> d h e"), S_sb[:], r=[('S', h) for h in range(8)]))
        P.run([t_ for t_ in outtoks if t_ is not None])
    return nc


_CACHE = {}


def kernel(x_prompt, x_sample, cache_att_k, cache_att_v, state_ret, state_conv, c_prompt, c_sample,
           w_ada, b_ada, g_pre1, w_in, rel_bias, w_br_ret, w_br_att, w_out, g_post1, g_pre2,
           w_up, conv_w, conv_b, w_down, g_post2):
    f = lambda a: np.ascontiguousarray(np.asarray(a, dtype=np.float32))
    x_prompt = f(x_prompt)
    B, S, _ = x_prompt.shape
    KEEP = min(512, S)
    if S not in _CACHE:
        _CACHE[S] = (build(S), _consts(S))
    nc, consts = _CACHE[S]
    shared = dict(w_ada=f(w_ada)[0], b_ada=f(b_ada)[0], g_pre1=f(g_pre1)[0], w_in=f(w_in)[0], rel_bias=f(rel_bias)[0],
                  w_br_ret=f(w_br_ret)[0], w_br_att=f(w_br_att)[0], w_out=f(w_out)[0], g_post1=f(g_post1)[0],
                  g_pre2=f(g_pre2)[0], w_up=f(w_up)[0], conv_w=f(conv_w)[0], conv_b=f(conv_b)[0], w_down=f(w_down)[0],
                  g_post2=f(g_post2)[0])
    shared.update(consts)
    x_sample = f(x_sample); ck = f(cache_att_k)[0]; cv = f(cache_att_v)[0]
    sr = f(state_ret)[0]; scv = f(state_conv)[0]; cp = f(c_prompt); cs = f(c_sample)
    in_maps = []
    for i in range(8):
        m = dict(shared)
        m["xp"] = x_prompt[i]
        m["xs"] = x_sample[2 * i:2 * i + 2].reshape(32, D)
        m["ck"] = ck[2 * i:2 * i + 2].reshape(2, 512, 1024)
        m["cv"] = cv[2 * i:2 * i + 2].reshape(2, 512, 1024)
        m["sret"] = sr[2 * i:2 * i + 2]
        m["sconv"] = scv[2 * i:2 * i + 2]
        m["c3"] = np.stack([cp[i], cs[2 * i], cs[2 * i + 1]])
        in_maps.append(m)
    res = run_bass_kernel_spmd(nc, in_maps, core_ids=list(range(8))).results
    g = lambda k: np.stack([np.asarray(r[k], dtype=np.float32) for r in res])
    yp = g("yp")
    ys = g("ys").reshape(16, 16, D)
    nkp = g("nkp").reshape(1, 8, KEEP, 8, 128)
    nvp = g("nvp").reshape(1, 8, KEEP, 8, 128)
    nrp = g("nrp").reshape(1, 8, 8, 128, 256)
    ncp = g("ncp").reshape(1, 8, 2, 2 * DFF)
    nks = g("nks").reshape(1, 16, 16, 8, 128)
    nvs = g("nvs").reshape(1, 16, 16, 8, 128)
    nrs = g("nrs").reshape(1, 16, 8, 128, 256)
    ncs = g("ncs").reshape(1, 16, 2, 2 * DFF)
    return (yp, ys, nkp, nvp, nrp, ncp, nks, nvs, nrs, ncs)
```

```python
import os
import numpy as np
from contextlib import ExitStack
import concourse.bass as bass
import concourse.mybir as mybir
from concourse.bass_utils import run_bass_kernel_spmd

F32 = mybir.dt.float32
BF16 = mybir.dt.bfloat16
U8 = mybir.dt.uint8
AF = mybir.ActivationFunctionType
ALU = mybir.AluOpType
AX = mybir.AxisListType

D = 2048
KC = 16
DFF = 5632
NFC = 44
PAST = 4096
EPS = 1e-6
NEG = -30000.0
C_RQ, C_RK, C_RV, C_RG, C_AQ, C_AK, C_AV, C_GR, C_GA = 0, 1024, 2048, 4096, 6144, 7168, 8192, 9216, 11264
ENGS = ('pe', 'act', 'dve', 'pool', 'sp')
NP = 8
NW = 4


_STOP = 99


class Prog:
    def __init__(self, nc, es):
        self.nc, self.es = nc, es
        self.q = {e: [] for e in ENGS}
        self.cnt = {e: 0 for e in ENGS}
        self.sem = {e: es.enter_context(nc.semaphore('s_' + e)) for e in ENGS}
        self.seen = {e: {} for e in ENGS}
        self.dsem = {q: [es.enter_context(nc.semaphore(f'd_{q}{i}')) for i in range(NP)] for q in ('pool', 'sp')}
        self.dcnt = {q: [0] * NP for q in ('pool', 'sp')}
        self.dnext = {'pool': 0, 'sp': 0}
        self.lastw = {}
        self.readers = {}
        self.nbank = 0
        self.dead = False
        self.pinned = set()

    def chk(self, k):
        if _STOP <= k:
            self.dead = True

    @staticmethod
    def _bx(r, w):
        rb = [k for k in r if isinstance(k, tuple) and k[0] == 'B']
        if not rb:
            return r, w
        return [k for k in r if not (isinstance(k, tuple) and k[0] == 'B')], list(w) + rb

    def _deps(self, eng, r, w):
        deps = {}

        def add(tok):
            if tok is None:
                return
            k, v = tok
            if eng == 'pe' and k == 'pe':
                return
            if deps.get(k, 0) < v:
                deps[k] = v
        for k in r:
            add(self.lastw.get(k))
        for k in w:
            add(self.lastw.get(k))
            for sk, v in self.readers.get(k, {}).items():
                add((sk, v))
        out = []
        for k, v in deps.items():
            if self.seen[eng].get(k, 0) >= v:
                continue
            self.seen[eng][k] = v
            out.append((k, v))
        return out

    def _commit(self, tok, r, w):
        for k in r:
            d = self.readers.setdefault(k, {})
            if d.get(tok[0], 0) < tok[1]:
                d[tok[0]] = tok[1]
        for k in w:
            self.lastw[k] = tok
            self.readers[k] = {}

    def op(self, eng, name, r=(), w=(), **kw):
        if self.dead:
            return None
        r, w = self._bx(r, w)
        waits = self._deps(eng, r, w)
        self.cnt[eng] += 1
        tok = (eng, self.cnt[eng])
        self.q[eng].append((waits, name, kw, 1, eng))
        self._commit(tok, r, w)
        return tok

    def mm(self, items, r=(), w=()):
        if self.dead:
            return None
        waits = self._deps('pe', r, w)
        self.cnt['pe'] += 1
        tok = ('pe', self.cnt['pe'])
        n = len(items)
        for i, (name, kw) in enumerate(items):
            self.q['pe'].append((waits if i == 0 else [], name, kw, 1 if i == n - 1 else 0, 'pe'))
        self._commit(tok, r, w)
        return tok

    def dma(self, q, out, in_, r=(), w=(), **kw):
        if self.dead:
            return None
        i = self.dnext[q]
        self.dnext[q] = (i + 1) % NP
        sk = ('d', q, i)
        waits = self._deps(q, r, w)
        prev = self.dcnt[q][i]
        if prev and self.seen[q].get(sk, 0) < prev:
            self.seen[q][sk] = prev
            waits.append((sk, prev))
        self.dcnt[q][i] = prev + 16
        tok = (sk, prev + 16)
        kw = dict(kw)
        kw['out'] = out
        kw['in_'] = in_
        self.q[q].append((waits, 'dma_start', kw, 16, sk))
        self._commit(tok, r, w)
        return tok

    def _semof(self, k):
        if isinstance(k, tuple):
            return self.dsem[k[1]][k[2]]
        return self.sem[k]

    def bank(self, avoid=(), pin=False):
        while (self.nbank % 8) in avoid or (self.nbank % 8) in self.pinned:
            self.nbank += 1
        b = self.nbank % 8
        self.nbank += 1
        if pin:
            self.pinned.add(b)
        return b

    def unpin(self, b):
        self.pinned.discard(b)

    def bank2(self):
        while (self.nbank % 2) or (self.nbank % 8) in self.pinned or ((self.nbank + 1) % 8) in self.pinned:
            self.nbank += 1
        b = self.nbank % 8
        self.nbank += 2
        return b

    def run(self, final):
        nc = self.nc
        with nc.Block() as block:
            def mk(en):
                def body(e):
                    for (waits, name, kw, inc, sk) in self.q[en]:
                        for (k, v) in waits:
                            e.wait_ge(self._semof(k), v)
                        ins = getattr(e, name)(**kw)
                        if inc:
                            ins.then_inc(self._semof(sk), inc)
                    if en == 'sp':
                        for (k, v) in final:
                            e.wait_ge(self._semof(k), v)
                return body
            block.tensor(mk('pe'))
            block.scalar(mk('act'))
            block.vector(mk('dve'))
            block.gpsimd(mk('pool'))
            block.sync(mk('sp'))


def _consts(S):
    half = 64
    inv = (10000.0 ** (-np.arange(half, dtype=np.float32) / half)).astype(np.float32)
    pos = np.concatenate([np.arange(S), PAST + np.arange(16)]).astype(np.float32)
    ang = (pos[:, None] * inv[None, :]).astype(np.float32)
    rc = np.cos(ang).astype(np.float32)
    rs = np.sin(ang).astype(np.float32)
    lg = np.log(1.0 - 2.0 ** (-5.0 - np.arange(8, dtype=np.float64)))
    sc = 128.0 ** -0.5

    def tabs(L):
        m = np.arange(L)[:, None, None]
        n = np.arange(L)[None, None, :]
        h = lg[None, :, None]
        dt = np.exp(h * (np.abs(n - m) - n - 1.0)) * sc
        dk = np.exp(lg[None, :] * (L - 1.0 - np.arange(L)[:, None])) * sc
        ep = EPS * np.exp(-2.0 * lg[None, :] * (np.arange(L)[:, None] + 1.0))
        ds = np.exp(L * lg)[None, :].repeat(128, 0)
        reps = 128 // L
        dt = np.tile(dt, (reps, 1, 1))
        dk = np.tile(dk, (reps, 1))
        ep = np.tile(ep, (reps, 1))
        dtp = np.zeros((128, 8, 64), np.float32)
        dtp[:, :, :L] = dt
        return dtp, dk.astype(np.float32), ep.astype(np.float32), ds.astype(np.float32)
    dp = tabs(64)
    dsm = tabs(16)
    return dict(rot_cos=rc, rot_sin=rs, dt_p=dp[0], dk_p=dp[1], ep_p=dp[2], ds_p=dp[3],
                dt_s=dsm[0], dk_s=dsm[1], ep_s=dsm[2], ds_s=dsm[3])


def build(S):
    NT = S // 256
    KEEP = min(512, S)
    nc = bass.Bass("TRN2", target_bir_lowering=False)

    def din(name, shape):
        return nc.dram_tensor(name, list(shape), F32, kind="ExternalInput").ap()

    def dout(name, shape):
        return nc.dram_tensor(name, list(shape), F32, kind="ExternalOutput").ap()
    xp = din("xp", [S, D]); xs = din("xs", [32, D])
    ck = din("ck", [2, 512, 1024]); cv = din("cv", [2, 512, 1024])
    sret = din("sret", [2, 8, 128, 256]); sconv = din("sconv", [2, 2, 2 * DFF]); c3 = din("c3", [3, D])
    w_ada = din("w_ada", [D, 6 * D]); b_ada = din("b_ada", [6 * D]); g_pre1 = din("g_pre1", [D])
    w_in = din("w_in", [D, 13312]); rel_bias = din("rel_bias", [8, 257])
    w_br_ret = din("w_br_ret", [D, D]); w_br_att = din("w_br_att", [1024, D]); w_out = din("w_out", [D, D])
    g_post1 = din("g_post1", [D]); g_pre2 = din("g_pre2", [D]); w_up = din("w_up", [D, 2 * DFF])
    conv_w = din("conv_w", [3, 2 * DFF]); conv_b = din("conv_b", [2 * DFF]); w_down = din("w_down", [DFF, D])
    g_post2 = din("g_post2", [D])
    rot_cos = din("rot_cos", [S + 16, 64]); rot_sin = din("rot_sin", [S + 16, 64])
    cdt = {k: din(k, [128, 8, 64]) for k in ("dt_p", "dt_s")}
    csm = {k: din(k, [128, 8]) for k in ("dk_p", "ep_p", "ds_p", "dk_s", "ep_s", "ds_s")}
    yp = dout("yp", [S, D]); ys = dout("ys", [32, D])
    nkp = dout("nkp", [KEEP, 1024]); nvp = dout("nvp", [KEEP, 1024])
    nrp = dout("nrp", [8, 128, 256]); ncp = dout("ncp", [2, 2 * DFF])
    nks = dout("nks", [32, 1024]); nvs = dout("nvs", [32, 1024])
    nrs = dout("nrs", [2, 8, 128, 256]); ncs = dout("ncs", [2, 2, 2 * DFF])

    es = ExitStack()
    with es:
        P = Prog(nc, es)

        def sb(name, shape, dt=F32):
            return es.enter_context(nc.sbuf_tensor(name, list(shape), dt))
        psum = es.enter_context(nc.psum_tensor("psum", [128, 4096], F32))

        def BK(b):
            return ('B', b)

        def bankf(b, n=512):
            return psum[:, b * 512: b * 512 + n]

        def bankb(b):
            return psum[:, b * 512:(b + 1) * 512].bitcast(BF16)

        ident_f = sb("ident_f", [128, 128]); ident_b = sb("ident_b", [128, 128], BF16)
        wring = [sb(f"wr{i}", [128, 8, 512], BF16) for i in range(NW)]
        x_sb = sb("x_sb", [128, 2, 2, D])
        akT = sb("akT", [128, 8, 768], BF16); av = sb("av", [128, 6, 1024], BF16)
        S_sb = sb("S_sb", [128, 8, 256]); Sb_sb = sb("Sb_sb", [128, 8, 256], BF16)
        bias_sb = sb("bias_sb", [128, 8, 256]); cb_sb = sb("cb_sb", [128, 8])
        gg_sb = sb("gg_sb", [128, 2, D])
        gpre = sb("gpre", [128, 2, 16]); badaT = sb("badaT", [128, 96]); cT = sb("cT", [128, 16, 3])
        scT = sb("scT", [128, 16, 3], BF16)
        modT = sb("modT", [128, 96, 3])
        gm = sb("gm", [128, 2, 3, 16]); sh = sb("sh", [128, 2, 3, 16])
        cw = sb("cw", [128, 3, 88]); cbT = sb("cbT", [128, 88])
        hist = sb("hist", [128, 88, 2])
        cos_sb = sb("cos_sb", [128, 2, 64]); sin_sb = sb("sin_sb", [128, 2, 64])
        dts = {k: sb("c_" + k, [128, 8, 64]) for k in cdt}
        sms = {k: sb("c_" + k, [128, 8]) for k in csm}
        st = sb("st", [128, 64])
        ARENA = 71680
        arena = sb("arena", [128, ARENA], U8)

        def AV(off, nbytes, dt, pat=None, **kw):
            v = arena[:, off:off + nbytes]
            if dt != U8:
                v = v.bitcast(dt)
            if pat:
                v = v.rearrange(pat, **kw)
            return v

        def AK(off, nbytes):
            return [('A', i) for i in range(off // 1024, (off + nbytes - 1) // 1024 + 1)]

        screp = AV(49152 + 2048, 4096, BF16, "p (a b) -> p a b", a=16); screpk = AK(49152 + 2048, 4096)
        bc1 = AV(49152 + 6144, 2048, F32); bc1k = AK(49152 + 6144, 2048)
        bc2 = AV(49152 + 8192, 2048, F32); bc2k = AK(49152 + 8192, 2048)
        wv = {}
        for nm, apx in (("w_ada", w_ada), ("w_in", w_in), ("w_br_ret", w_br_ret), ("w_br_att", w_br_att),
                        ("w_out", w_out), ("w_up", w_up), ("w_down", w_down)):
            wv[nm] = apx.rearrange("(kc p) f -> p kc f", p=128)
        wstate = {'n': 0}
        wsc = nc.dram_tensor("wsc", [140, 128, 4096], BF16).ap()
        wsc_idx = {}

        def gemm(*a, **k):
            for _ in gemm_gen(*a, **k):
                pass

        def gemm_gen(wname, nk, col0, feat=None, tok=None, ncols=512):
            npieces = (nk + 7) // 8
            nj = ncols // 128
            fb = [P.bank(pin=True) for _ in range(nj)] if feat else []
            tb = [P.bank(pin=True) for _ in tok['groups']] if tok else []
            for pi in range(npieces):
                k0 = pi * 8
                nkp = min(8, nk - k0)
                s = wstate['n'] % NW
                wstate['n'] += 1
                pkey = (wname, col0, pi)
                if wname == "w_ada":
                    P.dma('pool', wring[s][:, 0:nkp, 0:ncols], wv[wname][:, k0:k0 + nkp, col0:col0 + ncols], w=[('W', s)])
                elif pkey not in wsc_idx:
                    idx = len(wsc_idx)
                    wsc_idx[pkey] = idx
                    P.dma('pool', wring[s][:, 0:nkp, 0:ncols], wv[wname][:, k0:k0 + nkp, col0:col0 + ncols], w=[('W', s)])
                    P.dma('sp', wsc[idx, :, 0:nkp * 512].rearrange("p (k f) -> p k f", k=nkp), wring[s][:, 0:nkp, :],
                          r=[('W', s)], w=[('WS', idx)])
                else:
                    idx = wsc_idx[pkey]
                    P.dma('pool', wring[s][:, 0:nkp, :], wsc[idx, :, 0:nkp * 512].rearrange("p (k f) -> p k f", k=nkp),
                          r=[('WS', idx)], w=[('W', s)])
                if feat:
                    N = feat['N']
                    for j in range(nj):
                        items = []
                        for kl in range(nkp):
                            kc = k0 + kl
                            items.append(('matmul', dict(out=bankf(fb[j], N), lhsT=wring[s][:, kl, j * 128:(j + 1) * 128],
                                                         rhs=feat['act'][:, kc, 0:N], start=(kc == 0), stop=(kc == nk - 1))))
                        P.mm(items, r=[('W', s)] + feat['rkeys'], w=[BK(fb[j])])
                        yield
                if tok:
                    for gi, grp_ in enumerate(tok['groups']):
                        fn, M, rk = grp_[0], grp_[1], grp_[2]
                        if len(grp_) > 3:
                            rk = grp_[3](k0, nkp)
                        items = []
                        for kl in range(nkp):
                            kc = k0 + kl
                            items.append(('matmul', dict(out=psum[0:M, tb[gi] * 512: tb[gi] * 512 + ncols], lhsT=fn(kc),
                                                         rhs=wring[s][:, kl, 0:ncols], start=(kc == 0), stop=(kc == nk - 1))))
                        P.mm(items, r=[('W', s)] + rk, w=[BK(tb[gi])])
                        yield
            if feat:
                for j in range(nj):
                    feat['epi'](j, fb[j])
                    if not feat.get('hold'):
                        P.unpin(fb[j])
            if tok:
                for gi in range(len(tok['groups'])):
                    tok['epi'](gi, tb[gi])
                    P.unpin(tb[gi])

        outtoks = []
        P.op('pool', 'memset', w=['identf'], ap=ident_f[:], constant=0.0)
        P.op('pool', 'affine_select', r=['identf'], w=['identf'], out=ident_f[:], in_=ident_f[:], pattern=[[-1, 128]],
             compare_op=ALU.not_equal, fill=1.0, base=0, channel_multiplier=1)
        P.op('pool', 'tensor_copy', r=['identf'], w=['identb'], out=ident_b[:], in_=ident_f[:])
        P.op('pool', 'memset', w=['hist'], ap=hist[:], constant=0.0)
        P.op('pool', 'memset', w=['S'], ap=S_sb[:], constant=0.0)
        P.op('pool', 'memset', w=['Sb'], ap=Sb_sb[:], constant=0.0)
        nck = dict(allow_slow_non_contiguous=True)
        ldtmp = [AV(49152 + i * 512, 512, F32) for i in range(2)]
        ldk = [AK(49152, 1024)] * 2
        ldn = {'n': 0}

        def loadT(src2d, n, dst, dkeys, pat=None, **pkw):
            i = ldn['n'] % 2
            ldn['n'] += 1
            P.dma('sp', ldtmp[i][:n, :], src2d, w=ldk[i])
            b = P.bank()
            P.mm([('transpose', dict(out=psum[:, b * 512: b * 512 + n], in_=ldtmp[i][:n, :], identity=ident_f[:n, :n]))],
                 r=ldk[i] + ['identf'], w=[BK(b)])
            src = psum[:, b * 512: b * 512 + n]
            if pat:
                src = src.rearrange(pat, **pkw)
            P.op('dve', 'tensor_copy', r=[BK(b)], w=dkeys, out=dst, in_=src)

        loadT(g_pre1.rearrange("(c p) -> c p", p=128), 16, gpre[:, 0, :], ['gpre'])
        loadT(g_pre2.rearrange("(c p) -> c p", p=128), 16, gpre[:, 1, :], ['gpre'])
        loadT(b_ada.rearrange("(c p) -> c p", p=128), 96, badaT[:], ['badaT'])
        loadT(c3.rearrange("r (c p) -> (r c) p", p=128), 48, cT[:].rearrange("p c r -> p r c"), ['cT'],
              pat="p (r c) -> p r c", r=3)
        for j in range(3):
            loadT(conv_w[j].rearrange("(c p) -> c p", p=128), 88, cw[:, j, :], ['cw'])
        loadT(conv_b.rearrange("(c p) -> c p", p=128), 88, cbT[:], ['cbT'])
        for h in range(8):
            P.dma('sp', cb_sb[:, h:h + 1], rel_bias[h, 0:1].partition_broadcast(128), w=['cb'])
        for k in cdt:
            P.dma('sp', dts[k][:], cdt[k], w=['c_' + k])
        for k in csm:
            P.dma('sp', sms[k][:], csm[k], w=['c_' + k])
        P.op('dve', 'tensor_copy', r=['cb'], w=['bias'], out=bias_sb[:], in_=cb_sb[:, :].unsqueeze(2).broadcast_to([128, 8, 256]))
        for i in range(128):
            P.dma('sp', bias_sb[i:i + 1, :, i:256], rel_bias[:, 0:256 - i].unsqueeze(0), r=[], w=['bias'])
        P.op('dve', 'memset', w=['bias'], ap=bias_sb[0:64, :, 192:256], constant=NEG)
        P.chk(1)
        P.op('act', 'activation', r=['cT'], w=['scT'], out=scT[:], in_=cT[:], func=AF.Silu)

        def mod_epi(cb):
            def epi(j, b):
                ch = cb * 4 + j
                P.op('dve', 'tensor_scalar', r=[BK(b), 'badaT'], w=['modT'], out=modT[:, ch, :], in0=bankf(b, 3),
                     scalar1=badaT[:, ch:ch + 1], scalar2=None, op0=ALU.add)
            return epi

        def gg_block(cb, row, which, gpost):
            lc = (cb % 4) * 512
            P.dma('sp', bc1[:], b_ada[cb * 512:(cb + 1) * 512].partition_broadcast(128), w=bc1k)
            P.dma('sp', bc2[:], gpost[lc:lc + 512].partition_broadcast(128), w=bc2k)

            def epi(gi, b):
                P.op('dve', 'tensor_tensor', r=[BK(b)] + bc1k, w=['gg'], out=gg_sb[:, which, lc:lc + 512], in0=bankf(b),
                     in1=bc1[:], op=ALU.add)
                P.op('dve', 'tensor_tensor', r=['gg'] + bc2k, w=['gg'], out=gg_sb[:, which, lc:lc + 512],
                     in0=gg_sb[:, which, lc:lc + 512], in1=bc2[:], op=ALU.mult)
            return dict(groups=[(lambda kc: screp[:, kc, :], 128, screpk)], epi=epi)

        def set_screp(row):
            P.op('dve', 'tensor_copy', r=['scT'], w=screpk, out=screp[:, :, :],
                 in_=scT[:, :, row:row + 1].broadcast_to([128, 16, 128]))

        set_screp(0)
        for cb in range(24):
            tokspec = None
            if 8 <= cb < 12:
                tokspec = gg_block(cb, 0, 0, g_post1)
            elif 20 <= cb < 24:
                tokspec = gg_block(cb, 0, 1, g_post2)
            gemm("w_ada", 16, cb * 512, feat=dict(act=scT, N=3, rkeys=['scT'], epi=mod_epi(cb)), tok=tokspec)
        for r_ in range(3):
            for which, (c_sh, c_sc) in enumerate(((0, 16), (48, 64))):
                P.op('dve', 'scalar_tensor_tensor', r=['modT', 'gpre'], w=['gm'], out=gm[:, which, r_, :],
                     in0=modT[:, c_sc:c_sc + 16, r_], scalar=1.0, in1=gpre[:, which, :], op0=ALU.add, op1=ALU.mult)
                P.op('dve', 'tensor_copy', r=['modT'], w=['sh'], out=sh[:, which, r_, :], in_=modT[:, c_sh:c_sh + 16, r_])

        P.chk(2)
        uT_g = AV(0, 8192, BF16, "p (a b) -> p a b", a=16)
        uTk_g = AK(0, 8192)
        pro_done = set()

        def advance(gen, n):
            if gen is None:
                return
            for _ in range(n):
                try:
                    next(gen)
                except StopIteration:
                    return

        def norm_phase(R, G, row, xb, which, xn_off, filler=None, fstep=0):
            uT, uTk = uT_g, uTk_g
            xn = AV(xn_off, 8192, F32); xnk = AK(xn_off, 8192)
            junk = AV(xn_off + 8192, 4096, BF16); jk = AK(xn_off + 8192, 4096)
            for g in range(G):
                P.op('act', 'activation', r=[('x', xb, g)], w=jk + [('st', 0)], out=junk[:R, :], in_=x_sb[:R, xb, g, :], func=AF.Square,
                     accum_out=st[:R, 0:1])
                P.op('dve', 'tensor_scalar', r=[('st', 0)], w=[('st', 1)], out=st[:R, 1:2], in0=st[:R, 0:1], scalar1=1.0 / D,
                     scalar2=EPS, op0=ALU.mult, op1=ALU.add)
                P.op('act', 'activation', r=[('st', 1)], w=[('st', 2)], out=st[:R, 2:3], in_=st[:R, 1:2], func=AF.Sqrt)
                P.op('dve', 'reciprocal', r=[('st', 2)], w=[('st', 3)], out=st[:R, 3:4], in_=st[:R, 2:3])
                P.op('dve', 'tensor_scalar', r=[('x', xb, g), ('st', 3)], w=xnk, out=xn[:R, :], in0=x_sb[:R, xb, g, :], scalar1=st[:R, 3:4],
                     scalar2=None, op0=ALU.mult)
                for c4 in range(4):
                    advance(filler, fstep)
                    b = P.bank()
                    items = [('transpose', dict(out=psum[:, b * 512 + j * R: b * 512 + (j + 1) * R],
                                                in_=xn[:R, (c4 * 4 + j) * 128:(c4 * 4 + j + 1) * 128],
                                                identity=ident_f[:R, :R])) for j in range(4)]
                    P.mm(items, r=xnk + ['identf'], w=[BK(b)])
                    for j in range(4):
                        c = c4 * 4 + j
                        if j % 2 == 0:
                            P.op('act', 'activation', r=[BK(b), 'gm', 'sh'], w=uTk, out=uT[:, c, g * R:(g + 1) * R],
                                 in_=psum[:, b * 512 + j * R: b * 512 + (j + 1) * R], func=AF.Identity,
                                 scale=gm[:, which, row, c:c + 1], bias=sh[:, which, row, c:c + 1])
                        else:
                            P.op('dve', 'tensor_scalar', r=[BK(b), 'gm', 'sh'], w=uTk, out=uT[:, c, g * R:(g + 1) * R],
                                 in0=psum[:, b * 512 + j * R: b * 512 + (j + 1) * R], scalar1=gm[:, which, row, c:c + 1],
                                 scalar2=sh[:, which, row, c:c + 1], op0=ALU.mult, op1=ALU.add)

        def prologue(kind, t, xb, xn_off, filler=None, fstep=0):
            if kind == 'p':
                R, G, row, pos0 = 128, 2, 0, t * 256
            else:
                R, G, row, pos0 = 16, 1, 1 + t, S
            TT = G * R
            P.dma('sp', cos_sb[:R, 0:G, :], rot_cos[pos0:pos0 + TT, :].rearrange("(g p) j -> p g j", p=R), w=['cos'])
            P.dma('sp', sin_sb[:R, 0:G, :], rot_sin[pos0:pos0 + TT, :].rearrange("(g p) j -> p g j", p=R), w=['sin'])
            norm_phase(R, G, row, xb, 0, xn_off, filler, fstep)
            pro_done.add((kind, t))

        def load_x(kind, t, xb):
            if kind == 'p':
                src = xp[t * 256:(t + 1) * 256, :].rearrange("(g p) d -> p g d", p=128)
                P.dma('sp', x_sb[:, xb, 0:2, :], src, w=[('x', xb, g) for g in range(2)])
            else:
                src = xs[t * 16:(t + 1) * 16, :].rearrange("(g p) d -> p g d", p=16)
                P.dma('sp', x_sb[:16, xb, 0:1, :], src, w=[('x', xb, 0)])

        def tile(kind, t, xb, nxt):
            prompt = (kind == 'p')
            if prompt:
                R, G, L, CPG = 128, 2, 64, 2
                row = 0
                xsrc = xp[t * 256:(t + 1) * 256, :].rearrange("(g p) d -> p g d", p=128)
                ydst = yp[t * 256:(t + 1) * 256, :].rearrange("(g p) d -> p g d", p=128)
                pos0 = t * 256
                sfx = "_p"
                qg0 = 2 * t
            else:
                R, G, L, CPG = 16, 1, 16, 1
                row = 1 + t
                xsrc = xs[t * 16:(t + 1) * 16, :].rearrange("(g p) d -> p g d", p=16)
                ydst = ys[t * 16:(t + 1) * 16, :].rearrange("(g p) d -> p g d", p=16)
                pos0 = S + 0
                sfx = "_s"
            TT = G * R
            DT, DKT, EPT, DST = dts["dt" + sfx], sms["dk" + sfx], sms["ep" + sfx], sms["ds" + sfx]
            uT = AV(0, 8192, BF16, "p (a b) -> p a b", a=16); uTk = AK(0, 8192)
            qT = AV(8192, 4096, BF16, "p (a b) -> p a b", a=8); qTk = AK(8192, 4096)
            kT = AV(12288, 4096, BF16, "p (a b) -> p a b", a=8); kTk = AK(12288, 4096)
            kd = AV(16384, 4096, BF16, "p (a b) -> p a b", a=2); kdk = AK(16384, 4096)
            qkt = AV(20480, 2048, BF16, "p (a b) -> p a b", a=2); qktk2 = [AK(20480 + i * 1024, 1024) for i in range(2)]
            mretT = AV(8192, 16384, F32, "p (a b) -> p a b", a=16); mretk = AK(8192, 16384)
            fbuf = AV(8192, 16384, F32, "p (a b) -> p a b", a=2); fbufk = AK(8192, 16384)
            vbuf = [AV(24576 + i * 2048, 2048, BF16, "p (a b) -> p a b", a=2) for i in range(2)]
            vbk = [AK(24576 + i * 2048, 2048) for i in range(2)]
            sgbuf = [AV(28672 + i * 2048, 2048, BF16, "p (a b) -> p a b", a=2) for i in range(2)]
            sgk = [AK(28672 + i * 2048, 2048) for i in range(2)]
            aqT = AV(24576, 4096, BF16, "p (a b) -> p a b", a=8); aqTk = AK(24576, 4096)
            oaT = AV(28672, 4096, BF16, "p (a b) -> p a b", a=8); oaTk = AK(28672, 4096)
            goT = AV(32768, 8192, BF16, "p (a b) -> p a b", a=16); goTk = AK(32768, 8192)
            mgT = AV(40960, 8192, BF16, "p (a b) -> p a b", a=16); mgTk = AK(40960, 8192)
            hT = AV(49152, 22528, BF16, "p (a b) -> p a b", a=44); hTk = AK(49152, 22528)
            R5 = 49152

            def norm_to_uT(which):
                norm_phase(R, G, row, xb, which, R5)

            if nxt is not None:
                load_x(nxt[0], nxt[1], 1 - xb)
            if (kind, t) not in pro_done:
                prologue(kind, t, xb, R5)

            P.chk(3)

            def ugroups():
                return [((lambda kc, g=g: uT[:, kc, g * R:(g + 1) * R]), R, uTk) for g in range(G)]

            rot_deferred = []

            def rot_epi(isk, cb):
                def epi(g, b):
                    qf = AV(R5 + (g % 2) * 2048, 2048, F32); qfk = AK(R5 + (g % 2) * 2048, 2048)
                    bi = ((2 if isk else 0) + cb) * G + g
                    qkb = AV(32768 + bi * 1024, 1024, BF16); qktk = AK(32768 + bi * 1024, 1024)
                    t1 = AV(R5 + 4096, 1024, F32); t2 = AV(R5 + 5120, 1024, F32); tk = AK(R5 + 4096, 2048)
                    P.op('act', 'activation', r=[BK(b)], w=qfk, out=qf[:R, :], in_=bankf(b)[:R, :], func=AF.Copy)
                    q3 = qf[:R, :].rearrange("p (h d) -> p h d", h=4)
                    x1, x2 = q3[:, :, 0:64], q3[:, :, 64:128]
                    cosb = cos_sb[:R, g:g + 1, :].broadcast_to([R, 4, 64])
                    sinb = sin_sb[:R, g:g + 1, :].broadcast_to([R, 4, 64])
                    t13 = t1[:R, :].rearrange("p (h d) -> p h d", h=4)
                    t23 = t2[:R, :].rearrange("p (h d) -> p h d", h=4)
                    o3 = qkb[:R, :].rearrange("p (h d) -> p h d", h=4)
                    P.op('dve', 'tensor_tensor', r=qfk + ['cos'], w=tk, out=t13, in0=x1, in1=cosb, op=ALU.mult)
                    P.op('dve', 'tensor_tensor', r=qfk + ['sin'], w=tk, out=t23, in0=x2, in1=sinb, op=ALU.mult)
                    P.op('dve', 'tensor_tensor', r=tk, w=qktk, out=o3[:, :, 0:64], in0=t13, in1=t23, op=ALU.subtract)
                    P.op('dve', 'tensor_tensor', r=qfk + ['sin'], w=tk, out=t13, in0=x1, in1=sinb, op=ALU.mult)
                    P.op('dve', 'tensor_tensor', r=qfk + ['cos'], w=tk, out=t23, in0=x2, in1=cosb, op=ALU.mult)
                    P.op('dve', 'tensor_tensor', r=tk, w=qktk, out=o3[:, :, 64:128], in0=t13, in1=t23, op=ALU.add)
                    dstT, dk_ = (kT, kTk) if isk else (qT, qTk)

                    def later():
                        b2 = P.bank()
                        bb = bankb(b2)
                        items = [('transpose', dict(out=bb[:, j * R:(j + 1) * R], in_=qkb[:R, j * 128:(j + 1) * 128],
                                                    identity=ident_b[:R, :R])) for j in range(4)]
                        P.mm(items, r=qktk + ['identb'], w=[BK(b2)])
                        P.op('act', 'activation', r=[BK(b2)], w=dk_, out=dstT[:, cb * 4:(cb + 1) * 4, g * R:(g + 1) * R],
                             in_=bb[:, 0:4 * R].rearrange("p (j r) -> p j r", j=4), func=AF.Copy)
                    rot_deferred.append(later)
                    if isk:
                        P.op('dve', 'tensor_tensor', r=qktk + ['c_dk' + sfx], w=kdk,
                             out=kd[:R, g, cb * 512:(cb + 1) * 512].rearrange("p (h d) -> p h d", h=4), in0=o3,
                             in1=DKT[:R, cb * 4:(cb + 1) * 4].unsqueeze(2).broadcast_to([R, 4, 128]), op=ALU.mult)
                return epi
            for cb in range(2):
                gemm("w_in", 16, C_RQ + cb * 512, tok=dict(groups=ugroups(), epi=rot_epi(False, cb)))
            for cb in range(2):
                gemm("w_in", 16, C_RK + cb * 512, tok=dict(groups=ugroups(), epi=rot_epi(True, cb)))

            P.chk(4)
            def vg_gens(hp):
                vb, sgb = vbuf[hp % 2], sgbuf[hp % 2]
                vk, sk_ = vbk[hp % 2], sgk[hp % 2]

                def v_epi(g, b):
                    P.op('act', 'activation', r=[BK(b)], w=vk, out=vb[:R, g, :], in_=bankf(b)[:R, :], func=AF.Copy)

                def g_epi(g, b):
                    P.op('act', 'activation', r=[BK(b)], w=sk_, out=sgb[:R, g, :], in_=bankf(b)[:R, :], func=AF.Silu)
                yield from gemm_gen("w_in", 16, C_RV + hp * 512, tok=dict(groups=ugroups(), epi=v_epi))
                yield from gemm_gen("w_in", 16, C_RG + hp * 512, tok=dict(groups=ugroups(), epi=g_epi))

            def advance(gen, n):
                if gen is None:
                    return
                for _ in range(n):
                    try:
                        next(gen)
                    except StopIteration:
                        return

            advance(vg_gens(0), 10 ** 6)
            for fn_ in rot_deferred:
                fn_()
            for hp in range(4):
                vb, sgb = vbuf[hp % 2], sgbuf[hp % 2]
                vk, sk_ = vbk[hp % 2], sgk[hp % 2]
                filler = vg_gens(hp + 1) if hp < 3 else None
                sTd = AV(R5 + 8192, 256, BF16); sTdk = AK(R5 + 8192, 256)
                gtok = AV(R5 + 12288, 1024, BF16); gtokk = AK(R5 + 12288, 1024)
                for g in range(G):
                    bo = P.bank(pin=True)
                    for cc in range(CPG):
                        off = cc * L
                        c0 = g * R + off
                        bkvs = {}
                        for hh in range(2):
                            h = hp * 2 + hh
                            bs = P.bank()
                            P.mm([('matmul', dict(out=psum[off:off + L, bs * 512: bs * 512 + L], lhsT=kT[:, h, c0:c0 + L],
                                                  rhs=qT[:, h, c0:c0 + L], start=True, stop=True))], r=kTk + qTk, w=[BK(bs)])
                            P.op('dve', 'tensor_tensor', r=[BK(bs), 'c_dt' + sfx], w=sTdk, out=sTd[off:off + L, hh * 64: hh * 64 + L],
                                 in0=psum[off:off + L, bs * 512: bs * 512 + L], in1=DT[off:off + L, h, 0:L], op=ALU.mult)
                        for hh in range(2):
                            h = hp * 2 + hh
                            bkv = P.bank(pin=True)
                            bkvs[hh] = bkv
                            P.mm([('matmul', dict(out=psum[:, bkv * 512: bkv * 512 + 256], lhsT=kd[off:off + L, g, h * 128:(h + 1) * 128],
                                                  rhs=vb[off:off + L, g, hh * 256:(hh + 1) * 256], start=True, stop=True))],
                                 r=kdk + vk, w=[BK(bkv)])
                        advance(filler, 2)
                        for hh in range(2):
                            h = hp * 2 + hh
                            bkv = bkvs[hh]
                            P.mm([('matmul', dict(out=psum[off:off + L, bo * 512 + hh * 256: bo * 512 + (hh + 1) * 256],
                                                  lhsT=sTd[off:off + L, hh * 64: hh * 64 + L], rhs=vb[off:off + L, g, hh * 256:(hh + 1) * 256],
                                                  start=True, stop=False)),
                                  ('matmul', dict(out=psum[off:off + L, bo * 512 + hh * 256: bo * 512 + (hh + 1) * 256],
                                                  lhsT=qT[:, h, c0:c0 + L], rhs=Sb_sb[:, h, :], start=False, stop=True))],
                                 r=sTdk + vk + qTk + [('Sb', h)], w=[BK(bo)])
                            P.op('dve', 'scalar_tensor_tensor', r=[BK(bkv), ('S', h), 'c_ds' + sfx], w=[('S', h)], out=S_sb[:, h, :],
                                 in0=S_sb[:, h, :], scalar=DST[:, h:h + 1], in1=psum[:, bkv * 512: bkv * 512 + 256],
                                 op0=ALU.mult, op1=ALU.add)
                            P.unpin(bkv)
                            P.op('act', 'activation', r=[('S', h)], w=[('Sb', h)], out=Sb_sb[:, h, :], in_=S_sb[:, h, :], func=AF.Copy)
                    junk = AV(R5 + 16384, 512, BF16); jk = AK(R5 + 16384, 512)
                    for hh in range(2):
                        h = hp * 2 + hh
                        P.op('act', 'activation', r=[BK(bo)], w=jk + [('st', 8 + hh)], out=junk[:R, 0:256],
                             in_=psum[:R, bo * 512 + hh * 256: bo * 512 + (hh + 1) * 256], func=AF.Square, accum_out=st[:R, 8 + hh: 9 + hh])
                        P.op('dve', 'scalar_tensor_tensor', r=[('st', 8 + hh), 'c_ep' + sfx], w=[('st', 10 + hh)], out=st[:R, 10 + hh: 11 + hh],
                             in0=st[:R, 8 + hh: 9 + hh], scalar=1.0 / 256, in1=EPT[:R, h:h + 1], op0=ALU.mult, op1=ALU.add)
                        P.op('act', 'activation', r=[('st', 10 + hh)], w=[('st', 12 + hh)], out=st[:R, 12 + hh: 13 + hh],
                             in_=st[:R, 10 + hh: 11 + hh], func=AF.Sqrt)
                        P.op('dve', 'reciprocal', r=[('st', 12 + hh)], w=[('st', 14 + hh)], out=st[:R, 14 + hh: 15 + hh],
                             in_=st[:R, 12 + hh: 13 + hh])
                        P.op('dve', 'scalar_tensor_tensor', r=[BK(bo), ('st', 14 + hh)] + sk_, w=gtokk,
                             out=gtok[:R, hh * 256:(hh + 1) * 256], in0=psum[:R, bo * 512 + hh * 256: bo * 512 + (hh + 1) * 256],
                             scalar=st[:R, 14 + hh: 15 + hh], in1=sgb[:R, g, hh * 256:(hh + 1) * 256], op0=ALU.mult, op1=ALU.mult)
                    P.unpin(bo)
                    b2 = P.bank()
                    bb = bankb(b2)
                    items = [('transpose', dict(out=bb[:, j * R:(j + 1) * R], in_=gtok[:R, j * 128:(j + 1) * 128],
                                                identity=ident_b[:R, :R])) for j in range(4)]
                    P.mm(items, r=gtokk + ['identb'], w=[BK(b2)])
                    P.op('act', 'activation', r=[BK(b2)], w=goTk, out=goT[:, hp * 4:(hp + 1) * 4, g * R:(g + 1) * R],
                         in_=bb[:, 0:4 * R].rearrange("p (j r) -> p j r", j=4), func=AF.Copy)
                advance(filler, 10 ** 6)

            P.chk(5)
            sgt = [AV(R5 + i * 1024, 1024, F32) for i in range(2)]; sgtk = [AK(R5 + i * 1024, 1024) for i in range(2)]
            for fb_ in range(4):
                ybanks = {}

                def y_epi(j, b, ybanks=ybanks):
                    ybanks[j] = b

                def gr_epi(j, b, ybanks=ybanks, fb_=fb_):
                    c = fb_ * 4 + j
                    P.op('act', 'activation', r=[BK(b)], w=sgtk[j % 2], out=sgt[j % 2][:, 0:TT], in_=bankf(b, TT), func=AF.Sigmoid)
                    P.op('dve', 'tensor_tensor', r=[BK(ybanks[j])] + sgtk[j % 2], w=mretk, out=mretT[:, c, 0:TT],
                         in0=bankf(ybanks[j], TT), in1=sgt[j % 2][:, 0:TT], op=ALU.mult)
                    P.unpin(ybanks[j])
                gemm("w_br_ret", 16, fb_ * 512, feat=dict(act=goT, N=TT, rkeys=goTk, epi=y_epi, hold=True))
                gemm("w_in", 16, C_GR + fb_ * 512, feat=dict(act=uT, N=TT, rkeys=uTk, epi=gr_epi))

            P.chk(6)
            if prompt:
                slots = {}
                for g in range(G):
                    slots[g] = (qg0 + g) % 6
                keep_t = (t * 256 >= S - KEEP)
            else:
                slots = {0: 4}
                keep_t = True
                P.dma('pool', av[:, 0:4, :], cv[t].rearrange("(g p) f -> p g f", p=128), w=['av'])
                for gk in range(4):
                    ktmp = AV(R5, 2048, BF16); ktk = AK(R5, 2048)
                    P.dma('pool', ktmp[:, :], ck[t, gk * 128:(gk + 1) * 128, :], w=ktk)
                    for h4 in range(2):
                        b2 = P.bank()
                        bb = bankb(b2)
                        items = [('transpose', dict(out=bb[:, j * 128:(j + 1) * 128], in_=ktmp[:, (h4 * 4 + j) * 128:(h4 * 4 + j + 1) * 128],
                                                    identity=ident_b[:, :])) for j in range(4)]
                        P.mm(items, r=ktk + ['identb'], w=[BK(b2)])
                        P.op('act', 'activation', r=[BK(b2)], w=['akT'], out=akT[:, h4 * 4:(h4 + 1) * 4, gk * 128:(gk + 1) * 128],
                             in_=bb[:, 0:512].rearrange("p (j r) -> p j r", j=4), func=AF.Copy)

            def aq_epi(cb):
                def epi(j, b):
                    P.op('act', 'activation', r=[BK(b)], w=aqTk, out=aqT[:, cb * 4 + j, 0:TT], in_=bankf(b, TT), func=AF.Identity,
                         scale=float(128.0 ** -0.5))
                return epi

            def ak_epi(cb):
                def epi(j, b):
                    for g in range(G):
                        P.op('act', 'activation', r=[BK(b)], w=['akT'], out=akT[:, cb * 4 + j, slots[g] * 128: slots[g] * 128 + R],
                             in_=psum[:, b * 512 + g * R: b * 512 + (g + 1) * R], func=AF.Copy)
                return epi

            def kvout_epi(cb, dstp, dsts_):
                def epi(g, b):
                    tmp = AV(R5 + 4096 + (g % 2) * 2048, 2048, F32); tmk = AK(R5 + 4096 + (g % 2) * 2048, 2048)
                    P.op('dve', 'tensor_copy', r=[BK(b)], w=tmk, out=tmp[:R, :], in_=bankf(b)[:R, :])
                    if prompt:
                        r0 = t * 256 - (S - KEEP) + g * 128
                        dst = dstp[r0:r0 + 128, cb * 512:(cb + 1) * 512]
                    else:
                        dst = dsts_[t * 16:(t + 1) * 16, cb * 512:(cb + 1) * 512]
                    outtoks.append(P.dma('sp', dst, tmp[:R, :], r=tmk))
                return epi

            def av_epi(cb):
                kv = kvout_epi(cb, nvp, nvs)

                def epi(g, b):
                    P.op('act', 'activation', r=[BK(b)], w=['av'], out=av[:R, slots[g], cb * 512:(cb + 1) * 512], in_=bankf(b)[:R, :],
                         func=AF.Copy)
                    if keep_t:
                        kv(g, b)
                return epi
            for cb in range(2):
                gemm("w_in", 16, C_AQ + cb * 512, feat=dict(act=uT, N=TT, rkeys=uTk, epi=aq_epi(cb)))
            for cb in range(2):
                gemm("w_in", 16, C_AK + cb * 512, feat=dict(act=uT, N=TT, rkeys=uTk, epi=ak_epi(cb)),
                     tok=(dict(groups=ugroups(), epi=kvout_epi(cb, nkp, nks)) if keep_t else None))
            for cb in range(2):
                gemm("w_in", 16, C_AV + cb * 512, tok=dict(groups=ugroups(), epi=av_epi(cb)))

            P.chk(7)
            oats = [AV(R5 + i * 2048, 2048, BF16) for i in range(2)]
            oatks = [AK(R5 + i * 2048, 2048) for i in range(2)]
            items_gh = [(g, h) for g in range(G) for h in range(8)]
            stt_ = {}

            def kgs_of(g):
                if prompt:
                    qg = qg0 + g
                    return [(i, (qg - 4 + i) % 6, 128) for i in range(5) if qg - 4 + i >= 0], 640
                return [(i, i, 128) for i in range(4)] + [(4, 4, 16)], 528

            def stA(n):
                g, h = items_gh[n]
                kgs, jmax = kgs_of(g)
                c0 = kgs[0][0] * 128
                pb = n % 3
                sc_ = 16 + 4 * pb
                tt = AV(R5 + 4096 + pb * 2560, 2560, F32); ttk = AK(R5 + 4096 + pb * 2560, 2560)
                pp = AV(R5 + 12288 + pb * 1280, 1280, BF16); ppk = AK(R5 + 12288 + pb * 1280, 1280)
                b = P.bank2()
                items = [('matmul', dict(out=psum[:R, b * 512 + i * 128: b * 512 + i * 128 + n_], lhsT=aqT[:, h, g * R:(g + 1) * R],
                                         rhs=akT[:, h, sl * 128: sl * 128 + n_], start=True, stop=True)) for (i, sl, n_) in kgs]
                P.mm(items, r=aqTk + ['akT'], w=[BK(b), BK(b + 1)])
                sv = psum[:R, b * 512: b * 512 + 640]
                if c0 < 384:
                    P.op('dve', 'tensor_scalar', r=[BK(b), 'cb'], w=ttk, out=tt[:R, c0:384], in0=sv[:, c0:384],
                         scalar1=cb_sb[:R, h:h + 1], scalar2=None, op0=ALU.add)
                P.op('dve', 'tensor_tensor', r=[BK(b), BK(b + 1), 'bias'], w=ttk, out=tt[:R, 384:jmax], in0=sv[:, 384:jmax],
                     in1=bias_sb[:R, h, 0:jmax - 384], op=ALU.add)
                if prompt and c0 == 0:
                    P.op('dve', 'memset', w=ttk, ap=tt[64:128, 0:64], constant=NEG)
                P.op('dve', 'reduce_max', r=ttk, w=[('st', sc_)], out=st[:R, sc_:sc_ + 1], in_=tt[:R, c0:jmax], axis=AX.X)
                P.op('dve', 'tensor_scalar', r=[('st', sc_)], w=[('st', sc_ + 1)], out=st[:R, sc_ + 1:sc_ + 2], in0=st[:R, sc_:sc_ + 1],
                     scalar1=-1.0, scalar2=None, op0=ALU.mult)
                P.op('act', 'activation', r=ttk + [('st', sc_ + 1)], w=ppk + [('st', sc_ + 2)], out=pp[:R, c0:jmax], in_=tt[:R, c0:jmax],
                     func=AF.Exp, bias=st[:R, sc_ + 1:sc_ + 2], scale=1.0, accum_out=st[:R, sc_ + 2:sc_ + 3])
                P.op('dve', 'reciprocal', r=[('st', sc_ + 2)], w=[('st', sc_ + 3)], out=st[:R, sc_ + 3:sc_ + 4], in_=st[:R, sc_ + 2:sc_ + 3])
                stt_[n] = (kgs, pp, ppk, sc_)

            def stB(n):
                g, h = items_gh[n]
                kgs, pp, ppk, sc_ = stt_[n]
                pb = n % 2
                pT = AV(R5 + 16384 + pb * 2048, 1280, BF16, "p (a b) -> p a b", a=5); pTk = AK(R5 + 16384 + pb * 2048, 1280)
                b2 = P.bank()
                bb = bankb(b2)
                items = [('transpose', dict(out=bb[:n_, i * R:(i + 1) * R], in_=pp[:R, i * 128: i * 128 + n_],
                                            identity=ident_b[:R, :R])) for (i, sl, n_) in kgs]
                P.mm(items, r=ppk + ['identb'], w=[BK(b2)])
                i0 = kgs[0][0]
                P.op('act', 'activation', r=[BK(b2)], w=pTk, out=pT[:, i0:5, 0:R],
                     in_=bb[:, i0 * R: 5 * R].rearrange("p (a r) -> p a r", r=R), func=AF.Copy)
                stt_[n] = (kgs, pT, pTk, sc_)

            def stC(n):
                g, h = items_gh[n]
                kgs, pT, pTk, sc_ = stt_[n]
                oat, oatk = oats[g % 2], oatks[g % 2]
                b3 = P.bank()
                nk_ = len(kgs)
                items = [('matmul', dict(out=psum[:R, b3 * 512: b3 * 512 + 128], lhsT=pT[:n_, i, 0:R],
                                         rhs=av[:n_, sl, h * 128:(h + 1) * 128], start=(ii == 0), stop=(ii == nk_ - 1)))
                         for ii, (i, sl, n_) in enumerate(kgs)]
                P.mm(items, r=pTk + ['av'], w=[BK(b3)])
                P.op('act', 'activation', r=[BK(b3), ('st', sc_ + 3)], w=oatk, out=oat[:R, h * 128:(h + 1) * 128],
                     in_=psum[:R, b3 * 512: b3 * 512 + 128], func=AF.Identity, scale=st[:R, sc_ + 3:sc_ + 4])
                if h == 7:
                    for h4 in range(2):
                        b2 = P.bank()
                        bb = bankb(b2)
                        items = [('transpose', dict(out=bb[:, j * R:(j + 1) * R], in_=oat[:R, (h4 * 4 + j) * 128:(h4 * 4 + j + 1) * 128],
                                                    identity=ident_b[:R, :R])) for j in range(4)]
                        P.mm(items, r=oatk + ['identb'], w=[BK(b2)])
                        P.op('act', 'activation', r=[BK(b2)], w=oaTk, out=oaT[:, h4 * 4:(h4 + 1) * 4, g * R:(g + 1) * R],
                             in_=bb[:, 0:4 * R].rearrange("p (j r) -> p j r", j=4), func=AF.Copy)

            NI = len(items_gh)
            for i in range(NI + 3):
                if i >= 3:
                    stC(i - 3)
                if 2 <= i < NI + 2:
                    stB(i - 2)
                if i < NI:
                    stA(i)

            P.chk(8)
            tmp2 = [AV(R5 + 2048 + i * 1024, 1024, F32) for i in range(2)]; tmp2k = [AK(R5 + 2048 + i * 1024, 1024) for i in range(2)]
            for fb_ in range(4):
                ybanks = {}

                def y_epi2(j, b, ybanks=ybanks):
                    ybanks[j] = b

                def ga_epi(j, b, ybanks=ybanks, fb_=fb_):
                    c = fb_ * 4 + j
                    P.op('act', 'activation', r=[BK(b)], w=sgtk[j % 2], out=sgt[j % 2][:, 0:TT], in_=bankf(b, TT), func=AF.Sigmoid)
                    P.op('dve', 'tensor_tensor', r=[BK(ybanks[j])] + sgtk[j % 2], w=tmp2k[j % 2], out=tmp2[j % 2][:, 0:TT],
                         in0=bankf(ybanks[j], TT), in1=sgt[j % 2][:, 0:TT], op=ALU.mult)
                    P.unpin(ybanks[j])
                    P.op('dve', 'tensor_tensor', r=tmp2k[j % 2] + mretk, w=mgTk, out=mgT[:, c, 0:TT], in0=tmp2[j % 2][:, 0:TT],
                         in1=mretT[:, c, 0:TT], op=ALU.add)
                gemm("w_br_att", 8, fb_ * 512, feat=dict(act=oaT, N=TT, rkeys=oaTk, epi=y_epi2, hold=True))
                gemm("w_in", 16, C_GA + fb_ * 512, feat=dict(act=uT, N=TT, rkeys=uTk, epi=ga_epi))

            P.chk(9)
            def resid_gen(wname, nk, act, actk, which, buf, bufk, junk_off, rkfn=None):
                junk = AV(junk_off, 1024, BF16); jk = AK(junk_off, 1024)
                if rkfn is None:
                    groups = [((lambda kc, g=g: act[:, kc, g * R:(g + 1) * R]), R, actk) for g in range(G)]
                else:
                    groups = [((lambda kc, g=g: act[:, kc, g * R:(g + 1) * R]), R, actk, rkfn) for g in range(G)]

                def mk_epi(cb):
                    def epi(g, b):
                        col = 40 + g * 4 + cb
                        P.op('act', 'activation', r=[BK(b)], w=jk + [('st', col)], out=junk[:R, :], in_=bankf(b)[:R, :], func=AF.Square,
                             accum_out=st[:R, col:col + 1])
                        P.op('dve', 'tensor_copy', r=[BK(b)], w=bufk, out=buf[:R, g, cb * 512:(cb + 1) * 512], in_=bankf(b)[:R, :])
                    return epi
                for cb in range(4):
                    yield from gemm_gen(wname, nk, cb * 512, tok=dict(groups=groups, epi=mk_epi(cb)))
                for g in range(G):
                    c0_ = 40 + g * 4
                    P.op('dve', 'reduce_sum', r=[('st', c0_ + i) for i in range(4)], w=[('st', 48 + g)], out=st[:R, 48 + g: 49 + g],
                         in_=st[:R, c0_: c0_ + 4], axis=AX.X)
                    P.op('dve', 'tensor_scalar', r=[('st', 48 + g)], w=[('st', 50 + g)], out=st[:R, 50 + g: 51 + g], in0=st[:R, 48 + g: 49 + g],
                         scalar1=1.0 / D, scalar2=EPS, op0=ALU.mult, op1=ALU.add)
                    P.op('act', 'activation', r=[('st', 50 + g)], w=[('st', 52 + g)], out=st[:R, 52 + g: 53 + g], in_=st[:R, 50 + g: 51 + g],
                         func=AF.Sqrt)
                    P.op('dve', 'reciprocal', r=[('st', 52 + g)], w=[('st', 54 + g)], out=st[:R, 54 + g: 55 + g], in_=st[:R, 52 + g: 53 + g])
                    P.op('dve', 'scalar_tensor_tensor', r=bufk + [('st', 54 + g), 'gg'], w=bufk, out=buf[:R, g, :], in0=buf[:R, g, :],
                         scalar=st[:R, 54 + g: 55 + g], in1=gg_sb[:R, which, :], op0=ALU.mult, op1=ALU.mult)
                    P.op('dve', 'tensor_tensor', r=bufk + [('x', xb, g)], w=[('x', xb, g)], out=x_sb[:R, xb, g, :], in0=x_sb[:R, xb, g, :], in1=buf[:R, g, :],
                         op=ALU.add)

            def resid_phase(*a_, **k_):
                advance(resid_gen(*a_, **k_), 10 ** 9)

            h1buf = AV(R5, 16384, F32, "p (a b) -> p a b", a=2); h1k = AK(R5, 16384)
            resid_phase("w_out", 16, mgT, mgTk, 0, h1buf, h1k, R5 + 16384)
            norm_to_uT(1)

            P.chk(10)
            GO = 12288
            upb = [AV(GO + i * 4608, 4 * (TT + 2) * 4, F32, "p (c n) -> p c n", c=4) for i in range(2)]
            upk = [AK(GO, 4608), AK(GO + 4608, 4608)]
            hvbs = [AV(GO + 9216 + i * 4096, 4 * TT * 4, F32, "p (c n) -> p c n", c=4) for i in range(2)]
            hvks = [[AK(GO + 9216 + i * 4096 + j * 1024, 1024) for j in range(4)] for i in range(2)]
            hgbs = [AV(GO + 17408 + i * 4096, 4 * TT * 4, F32, "p (c n) -> p c n", c=4) for i in range(2)]
            hgks = [[AK(GO + 17408 + i * 4096 + j * 1024, 1024) for j in range(4)] for i in range(2)]
            gtb = AV(GO + 25600, 4 * TT * 4, F32, "p (c n) -> p c n", c=4); gtk = AK(GO + 25600, 4096)
            if not prompt:
                for jj in range(2):
                    loadT(sconv[t, jj].rearrange("(c p) -> c p", p=128), 88, hist[:, :, jj], ['hist'])
            last_tile = (prompt and t == NT - 1) or (not prompt)

            def conv_half(isg, blk):
                ub, uk = upb[isg], upk[isg]
                hb = (hgbs if isg else hvbs)[blk % 2]
                hks = (hgks if isg else hvks)[blk % 2]
                cbase = (NFC if isg else 0) + blk * 4
                P.op('dve', 'tensor_copy', r=['hist'], w=uk, out=ub[:, :, 0:2], in_=hist[:, cbase:cbase + 4, :])

                def epi(j, b):
                    P.op('act', 'activation', r=[BK(b)], w=uk, out=ub[:, j, 2:2 + TT], in_=bankf(b, TT), func=AF.Copy)
                    if j < 3:
                        return
                    for jj in range(4):
                        c = cbase + jj
                        hk = hks[jj]
                        P.op('act', 'activation', r=uk + ['cw', 'cbT'], w=hk, out=hb[:, jj, 0:TT], in_=ub[:, jj, 2:2 + TT], func=AF.Identity,
                             scale=cw[:, 2, c:c + 1], bias=cbT[:, c:c + 1])
                        P.op('dve', 'scalar_tensor_tensor', r=uk + hk + ['cw'], w=hk, out=hb[:, jj, 0:TT], in0=ub[:, jj, 1:1 + TT],
                             scalar=cw[:, 1, c:c + 1], in1=hb[:, jj, 0:TT], op0=ALU.mult, op1=ALU.add)
                        P.op('dve', 'scalar_tensor_tensor', r=uk + hk + ['cw'], w=hk, out=hb[:, jj, 0:TT], in0=ub[:, jj, 0:TT],
                             scalar=cw[:, 0, c:c + 1], in1=hb[:, jj, 0:TT], op0=ALU.mult, op1=ALU.add)
                    P.op('dve', 'tensor_copy', r=uk, w=['hist'], out=hist[:, cbase:cbase + 4, :], in_=ub[:, :, TT:TT + 2])
                return epi

            def cstate_tok(col0):
                if not last_tile:
                    return None
                if prompt:
                    grp = [((lambda kc: uT[:, kc, TT - 2:TT]), 2, uTk)]
                else:
                    grp = [((lambda kc: uT[:, kc, 0:16]), 16, uTk)]

                def epi(gi, b):
                    tmp = AV(8192 + ((col0 // 512) % 2) * 2048, 2048, F32); tmk = AK(8192 + ((col0 // 512) % 2) * 2048, 2048)
                    if prompt:
                        P.op('dve', 'tensor_copy', r=[BK(b)], w=tmk, out=tmp[0:2, :], in_=bankf(b)[0:2, :])
                        outtoks.append(P.dma('sp', ncp[:, col0:col0 + 512], tmp[0:2, :], r=tmk))
                    else:
                        P.op('dve', 'tensor_copy', r=[BK(b)], w=tmk, out=tmp[0:16, :], in_=bankf(b)[0:16, :])
                        outtoks.append(P.dma('sp', ncs[t, :, col0:col0 + 512], tmp[14:16, :], r=tmk))
                return dict(groups=grp, epi=epi)

            def gelu_mul(blk):
                hgb, hvb = hgbs[blk % 2], hvbs[blk % 2]
                hgk = sum(hgks[blk % 2], []); hvk = sum(hvks[blk % 2], [])
                P.op('act', 'activation', r=hgk, w=gtk, out=gtb[:, :, 0:TT], in_=hgb[:, :, 0:TT], func=AF.Square)
                P.op('dve', 'tensor_scalar', r=gtk, w=gtk, out=gtb[:, :, 0:TT], in0=gtb[:, :, 0:TT], scalar1=0.044715, scalar2=1.0,
                     op0=ALU.mult, op1=ALU.add)
                P.op('dve', 'tensor_tensor', r=gtk + hgk, w=gtk, out=gtb[:, :, 0:TT], in0=gtb[:, :, 0:TT], in1=hgb[:, :, 0:TT], op=ALU.mult)
                P.op('act', 'activation', r=gtk, w=gtk, out=gtb[:, :, 0:TT], in_=gtb[:, :, 0:TT], func=AF.Sigmoid, scale=1.5957691216057308)
                P.op('dve', 'tensor_tensor', r=gtk + hgk, w=gtk, out=gtb[:, :, 0:TT], in0=gtb[:, :, 0:TT], in1=hgb[:, :, 0:TT], op=ALU.mult)
                P.op('dve', 'tensor_tensor', r=gtk + hvk, w=AK(49152 + blk * 2048, 2048), out=hT[:, blk * 4:(blk + 1) * 4, 0:TT],
                     in0=gtb[:, :, 0:TT], in1=hvb[:, :, 0:TT], op=ALU.mult)

            for blk in range(11):
                gemm("w_up", 16, blk * 512, feat=dict(act=uT, N=TT, rkeys=uTk, epi=conv_half(0, blk)), tok=cstate_tok(blk * 512))
                gemm("w_up", 16, DFF + blk * 512, feat=dict(act=uT, N=TT, rkeys=uTk, epi=conv_half(1, blk)),
                     tok=cstate_tok(DFF + blk * 512))
                if blk >= 1:
                    gelu_mul(blk - 1)
            gelu_mul(10)

            P.chk(11)
            wd = resid_gen("w_down", NFC, hT, hTk, 1, fbuf, fbufk, 24576, rkfn=lambda k0, nkp: AK(49152 + k0 * 512, nkp * 512))
            if nxt is not None:
                advance(wd, 6)
                prologue(nxt[0], nxt[1], 1 - xb, 28672, filler=wd, fstep=4)
            advance(wd, 10 ** 9)
            outtoks.append(P.dma('sp', ydst, x_sb[:R, xb, 0:G, :], r=[('x', xb, g) for g in range(G)]))

        seq = [('p', t) for t in range(NT)] + [('s', 0), ('s', 1)]
        load_x('p', 0, 0)
        for t in range(NT):
            tile('p', t, t % 2, seq[t + 1])
        outtoks.append(P.dma('sp', nrp.rearrange("h d e -> d h e"), S_sb[:], r=[('S', h) for h in range(8)]))
        for b_ in range(2):
            set_screp(1 + b_)
            for cb in (8, 9, 10, 11):
                gemm("w_ada", 16, cb * 512, tok=gg_block(cb, 1 + b_, 0, g_post1))
            for cb in (20, 21, 22, 23):
                gemm("w_ada", 16, cb * 512, tok=gg_block(cb, 1 + b_, 1, g_post2))
            P.dma('sp', S_sb[:], sret[b_].rearrange("h d e -> d h e"), w=[('S', h) for h in range(8)])
            for h in range(8):
                P.op('act', 'activation', r=[('S', h)], w=[('Sb', h)], out=Sb_sb[:, h, :], in_=S_sb[:, h, :], func=AF.Copy)
            tile('s', b_, (NT + b_) % 2, seq[NT + b_ + 1] if NT + b_ + 1 < len(seq) else None)
            outtoks.append(P.dma('sp', nrs[b_].rearrange("h d e -> d h e"), S_sb[:], r=[('S', h) for h in range(8)]))
        P.run([t_ for t_ in outtoks if t_ is not None])
    return nc


_CACHE = {}


def kernel(x_prompt, x_sample, cache_att_k, cache_att_v, state_ret, state_conv, c_prompt, c_sample,
           w_ada, b_ada, g_pre1, w_in, rel_bias, w_br_ret, w_br_att, w_out, g_post1, g_pre2,
           w_up, conv_w, conv_b, w_down, g_post2):
    f = lambda a: np.ascontiguousarray(np.asarray(a, dtype=np.float32))
    x_prompt = f(x_prompt)
    B, S, _ = x_prompt.shape
    KEEP = min(512, S)
    if S not in _CACHE:
        _CACHE[S] = (build(S), _consts(S))
    nc, consts = _CACHE[S]
    shared = dict(w_ada=f(w_ada)[0], b_ada=f(b_ada)[0], g_pre1=f(g_pre1)[0], w_in=f(w_in)[0], rel_bias=f(rel_bias)[0],
                  w_br_ret=f(w_br_ret)[0], w_br_att=f(w_br_att)[0], w_out=f(w_out)[0], g_post1=f(g_post1)[0],
                  g_pre2=f(g_pre2)[0], w_up=f(w_up)[0], conv_w=f(conv_w)[0], conv_b=f(conv_b)[0], w_down=f(w_down)[0],
                  g_post2=f(g_post2)[0])
    shared.update(consts)
    x_sample = f(x_sample); ck = f(cache_att_k)[0]; cv = f(cache_att_v)[0]
    sr = f(state_ret)[0]; scv = f(state_conv)[0]; cp = f(c_prompt); cs = f(c_sample)
    in_maps = []
    for i in range(8):
        m = dict(shared)
        m["xp"] = x_prompt[i]
        m["xs"] = x_sample[2 * i:2 * i + 2].reshape(32, D)
        m["ck"] = ck[2 * i:2 * i + 2].reshape(2, 512, 1024)
        m["cv"] = cv[2 * i:2 * i + 2].reshape(2, 512, 1024)
        m["sret"] = sr[2 * i:2 * i + 2]
        m["sconv"] = scv[2 * i:2 * i + 2]
        m["c3"] = np.stack([cp[i], cs[2 * i], cs[2 * i + 1]])
        in_maps.append(m)
    res = run_bass_kernel_spmd(nc, in_maps, core_ids=list(range(8))).results
    g = lambda k: np.stack([np.asarray(r[k], dtype=np.float32) for r in res])
    yp = g("yp")
    ys = g("ys").reshape(16, 16, D)
    nkp = g("nkp").reshape(1, 8, KEEP, 8, 128)
    nvp = g("nvp").reshape(1, 8, KEEP, 8, 128)
    nrp = g("nrp").reshape(1, 8, 8, 128, 256)
    ncp = g("ncp").reshape(1, 8, 2, 2 * DFF)
    nks = g("nks").reshape(1, 16, 16, 8, 128)
    nvs = g("nvs").reshape(1, 16, 16, 8, 128)
    nrs = g("nrs").reshape(1, 16, 8, 128, 256)
    ncs = g("ncs").reshape(1, 16, 2, 2 * DFF)
    return (yp, ys, nkp, nvp, nrp, ncp, nks, nvs, nrs, ncs)
```
